# Optimizing a Trainium2 kernel written in Bass

```python
import jax, jax.numpy as jnp
from jax import lax
import numpy as np

D_MODEL = 2048
BATCH = 1
SEQ = 8192
DEPTH = 1

EPS = 1e-6
MEM_LEN = 256

CHUNK = 128
A_GROUP_DIM = 128
A_GROUPS = D_MODEL // A_GROUP_DIM
A_WIDTH = A_GROUPS * A_GROUP_DIM

QK_NOPE = 128
QK_ROPE = 64
V_DIM = 128
MLA_HEADS = D_MODEL // V_DIM
Q_LORA = 512
KV_LORA = 512
MLA_WIDTH = MLA_HEADS * V_DIM
QK_DIM = QK_NOPE + QK_ROPE
ROPE_THETA = 10000.0
Q_BLOCK = 128

MEM_HEADS = 4
MEM_HEAD_DIM = D_MODEL // MEM_HEADS
MEM_WIDTH = MEM_HEADS * MEM_HEAD_DIM

N_BRANCH = 3
BRANCH_WIDTH = D_MODEL

IN_SIZES = (A_WIDTH, A_WIDTH, A_WIDTH,
            Q_LORA, KV_LORA, QK_ROPE, MLA_WIDTH,
            MEM_WIDTH, MEM_WIDTH)
IN_TOTAL = int(sum(IN_SIZES))
IN_SPLITS = [int(o) for o in np.cumsum(IN_SIZES)[:-1]]

kernel_name = "hybrid_gmlp_mla_memory_gated"


def rmsnorm(x, g):
    xf = x.astype(jnp.float32)
    xf = xf * lax.rsqrt(jnp.mean(xf * xf, axis=-1, keepdims=True) + EPS)
    return xf.astype(x.dtype) * g


def layernorm(x, g, b):
    xf = x.astype(jnp.float32)
    mu = jnp.mean(xf, axis=-1, keepdims=True)
    var = jnp.mean(jnp.square(xf - mu), axis=-1, keepdims=True)
    return ((xf - mu) * lax.rsqrt(var + EPS)).astype(x.dtype) * g + b


def rope_tables(positions):
    inv_freq = 1.0 / (ROPE_THETA ** (jnp.arange(0, QK_ROPE, 2, dtype=jnp.float32) / QK_ROPE))
    ang = positions.astype(jnp.float32)[..., None] * inv_freq
    return jnp.cos(ang), jnp.sin(ang)


def apply_rope(t, cos, sin):
    t1, t2 = jnp.split(t, 2, axis=-1)
    cos = cos.astype(t.dtype)
    sin = sin.astype(t.dtype)
    return jnp.concatenate([t1 * cos - t2 * sin, t2 * cos + t1 * sin], axis=-1)


def chunked_spatial_gating(u_raw, v_raw, ln_g, ln_b, w_s, b_s):
    B, S, _ = u_raw.shape
    u = jax.nn.gelu(u_raw)
    v = layernorm(jax.nn.gelu(v_raw), ln_g, ln_b)
    vc = v.reshape(B, S // CHUNK, CHUNK, A_GROUPS, A_GROUP_DIM)
    causal = jnp.tril(jnp.ones((CHUNK, CHUNK), dtype=w_s.dtype))
    ws = w_s * causal[None]
    sv = jnp.einsum('gts,bcsgd->bctgd', ws, vc) + b_s.T[None, None, :, :, None]
    return u * sv.reshape(B, S, A_WIDTH)


def latent_attention(c_q, c_kv, k_rope, cos, sin, q_norm_g, w_uq, kv_norm_g, w_ukv):
    B, S, _ = c_q.shape
    q = (rmsnorm(c_q, q_norm_g) @ w_uq).reshape(B, S, MLA_HEADS, QK_DIM)
    q_nope, q_pe = jnp.split(q, [QK_NOPE], axis=-1)
    q_pe = apply_rope(q_pe, cos[:, :, None, :], sin[:, :, None, :])
    q = jnp.concatenate([q_nope, q_pe], axis=-1)

    kv = (rmsnorm(c_kv, kv_norm_g) @ w_ukv).reshape(B, S, MLA_HEADS, QK_NOPE + V_DIM)
    k_nope, v = jnp.split(kv, [QK_NOPE], axis=-1)
    k_pe = apply_rope(k_rope, cos, sin)
    k_pe = jnp.broadcast_to(k_pe[:, :, None, :], (B, S, MLA_HEADS, QK_ROPE))
    k = jnp.concatenate([k_nope, k_pe], axis=-1)

    qh = q.transpose(0, 2, 1, 3)
    kh = k.transpose(0, 2, 1, 3)
    vh = v.transpose(0, 2, 1, 3)
    n_blk = S // Q_BLOCK
    q_blocks = qh.reshape(B, MLA_HEADS, n_blk, Q_BLOCK, QK_DIM).transpose(2, 0, 1, 3, 4)
    scale = QK_DIM ** -0.5
    key_pos = jnp.arange(S)

    def one_block(args):
        qi, bi = args
        s = jnp.einsum('bhqd,bhkd->bhqk', qi, kh).astype(jnp.float32) * scale
        q_pos = bi * Q_BLOCK + jnp.arange(Q_BLOCK)
        mask = key_pos[None, :] <= q_pos[:, None]
        s = jnp.where(mask[None, None], s, -1e30)
        p = jax.nn.softmax(s, axis=-1).astype(vh.dtype)
        return jnp.einsum('bhqk,bhkd->bhqd', p, vh)

    o = lax.map(one_block, (q_blocks, jnp.arange(n_blk)))
    return o.transpose(1, 0, 3, 2, 4).reshape(B, S, MLA_WIDTH)


def memory_attention(q_m, mem, mem_norm_g, w_mem_kv):
    B, S, _ = q_m.shape
    kv = rmsnorm(mem, mem_norm_g) @ w_mem_kv
    k_m, v_m = jnp.split(kv.reshape(B, MEM_LEN, 2, MEM_HEADS, MEM_HEAD_DIM), 2, axis=2)
    k_m, v_m = k_m[:, :, 0], v_m[:, :, 0]
    q = q_m.reshape(B, S, MEM_HEADS, MEM_HEAD_DIM)
    s = jnp.einsum('bshd,bmhd->bhsm', q, k_m).astype(jnp.float32) * (MEM_HEAD_DIM ** -0.5)
    p = jax.nn.softmax(s, axis=-1).astype(v_m.dtype)
    return jnp.einsum('bhsm,bmhd->bshd', p, v_m).reshape(B, S, MEM_WIDTH)


def hybrid_layer(x, mem, cos, sin, g_pre, w_in, a_ln_g, a_ln_b, a_w_s, a_b_s,
                 q_norm_g, w_uq, kv_norm_g, w_ukv, mem_norm_g, w_mem_kv,
                 w_gate, b_gate, w_branch, w_out, g_post):
    B, S, D = x.shape
    h = rmsnorm(x, g_pre)
    proj = h @ w_in
    u, v, z_a, c_q, c_kv, k_rope, z_b, q_m, z_m = jnp.split(proj, IN_SPLITS, axis=-1)

    y_a = chunked_spatial_gating(u, v, a_ln_g, a_ln_b, a_w_s, a_b_s) * jax.nn.silu(z_a)
    y_b = latent_attention(c_q, c_kv, k_rope, cos, sin, q_norm_g, w_uq, kv_norm_g, w_ukv) * jax.nn.silu(z_b)
    y_m = memory_attention(q_m, mem, mem_norm_g, w_mem_kv) * jax.nn.silu(z_m)

    y = jnp.stack([y_a, y_b, y_m], axis=2)
    p = jnp.einsum('bsnc,ncd->bsnd', y, w_branch)
    gates = jax.nn.sigmoid(h @ w_gate + b_gate).reshape(B, S, N_BRANCH, D)
    merged = jnp.sum(gates * p, axis=2)
    out = merged @ w_out
    return x + rmsnorm(out, g_post)


def setup_inputs(seed: int = 0) -> dict:
    key = jax.random.key(seed)
    ks = jax.random.split(key, 24)
    f32 = jnp.float32
    L, D = DEPTH, D_MODEL

    def w(k, shape, fan_in):
        return jax.random.normal(k, shape, f32) * (fan_in ** -0.5)

    def gain(k, shape):
        return 1.0 + 0.02 * jax.random.normal(k, shape, f32)

    def bias(k, shape):
        return 0.01 * jax.random.normal(k, shape, f32)

    return {
        "x": jax.random.normal(ks[0], (BATCH, SEQ, D), f32),
        "mem": jax.random.normal(ks[1], (BATCH, MEM_LEN, D), f32),
        "positions": jnp.broadcast_to(jnp.arange(SEQ, dtype=jnp.int32)[None], (BATCH, SEQ)),
        "g_pre": gain(ks[2], (L, D)),
        "w_in": w(ks[3], (L, D, IN_TOTAL), D),
        "a_ln_g": gain(ks[4], (L, A_WIDTH)),
        "a_ln_b": bias(ks[5], (L, A_WIDTH)),
        "a_w_s": w(ks[6], (L, A_GROUPS, CHUNK, CHUNK), CHUNK),
        "a_b_s": gain(ks[7], (L, A_GROUPS, CHUNK)),
        "q_norm_g": gain(ks[8], (L, Q_LORA)),
        "w_uq": w(ks[9], (L, Q_LORA, MLA_HEADS * QK_DIM), Q_LORA),
        "kv_norm_g": gain(ks[10], (L, KV_LORA)),
        "w_ukv": w(ks[11], (L, KV_LORA, MLA_HEADS * (QK_NOPE + V_DIM)), KV_LORA),
        "mem_norm_g": gain(ks[12], (L, D)),
        "w_mem_kv": w(ks[13], (L, D, 2 * MEM_WIDTH), D),
        "w_gate": w(ks[14], (L, D, N_BRANCH * D), D),
        "b_gate": bias(ks[15], (L, N_BRANCH * D)),
        "w_branch": w(ks[16], (L, N_BRANCH, BRANCH_WIDTH, D), BRANCH_WIDTH),
        "w_out": w(ks[17], (L, D, D), D),
        "g_post": gain(ks[18], (L, D)),
    }


def reference(x, mem, positions, g_pre, w_in, a_ln_g, a_ln_b, a_w_s, a_b_s,
              q_norm_g, w_uq, kv_norm_g, w_ukv, mem_norm_g, w_mem_kv,
              w_gate, b_gate, w_branch, w_out, g_post):
    cos, sin = rope_tables(positions)
    for l in range(DEPTH):
        x = hybrid_layer(x, mem, cos, sin, g_pre[l], w_in[l], a_ln_g[l], a_ln_b[l],
                         a_w_s[l], a_b_s[l], q_norm_g[l], w_uq[l], kv_norm_g[l], w_ukv[l],
                         mem_norm_g[l], w_mem_kv[l], w_gate[l], b_gate[l], w_branch[l],
                         w_out[l], g_post[l])
    return x
```

```python
import numpy as np
import concourse.bass as bass
import concourse.mybir as mybir
from concourse.bass_utils import run_bass_kernel_spmd

F32 = mybir.dt.float32
BF16 = mybir.dt.bfloat16
I32 = mybir.dt.int32
AF = mybir.ActivationFunctionType
ALU = mybir.AluOpType
AX = mybir.AxisListType

P = 128
S = 8192
D = 2048
NCORE = 8
TOK = 1024
EPS = 1e-6
PIECE = 2048
TWO_PI = float(2.0 * np.pi)
C1 = 6.28125
C2 = float(2.0 * np.pi - 6.28125)

PC = {}
_n = 0
for _name, _cnt in (("ckv", 6), ("cq", 4), ("head", 16), ("zb", 16), ("u", 16), ("za", 16),
                    ("v", 16), ("memk", 16), ("memv", 16), ("qm", 16), ("zm", 16),
                    ("gate", 48), ("branch", 48), ("out", 16)):
    PC[_name] = _n
    _n += _cnt
NPIECE = _n

SM_GPRE, SM_QNG, SM_KVG, SM_MEMG, SM_BGATE, SM_CST = 0, 16, 20, 24, 40, 88
SM_N = 96
ROW_LNG, ROW_LNB, ROW_BS, ROW_GPOST = 0, 2048, 4096, 6144


def _fm_pieces(W):
    K, C = W.shape
    kc = K // P
    n = C // P
    t = W.reshape(kc, P, n, P).transpose(2, 1, 0, 3)
    if kc == 16:
        return np.ascontiguousarray(t).reshape(n, P, PIECE)
    raise ValueError


def _tm_pieces(W):
    K, C = W.shape
    g = C // 512
    t = W.reshape(4, 4, P, g, 512).transpose(3, 0, 2, 1, 4)
    return np.ascontiguousarray(t).reshape(g * 4, P, PIECE)


def prep_inputs(inp):
    f = np.float32
    x = np.asarray(inp["x"], f)[0]
    mem = np.asarray(inp["mem"], f)[0]
    pos = np.asarray(inp["positions"], np.int32)
    w_in = np.asarray(inp["w_in"], f)[0]
    o = np.cumsum([0, 2048, 2048, 2048, 512, 512, 64, 2048, 2048, 2048])
    wu, wv, wza, wcq, wckv, wkr, wzb, wqm, wzm = [w_in[:, o[i]:o[i + 1]] for i in range(9)]
    sw = np.concatenate([np.arange(32, 64), np.arange(0, 32)])
    pieces = np.empty((NPIECE, P, PIECE), f)
    ckv_cols = np.concatenate([wckv, wkr, wkr, wkr[:, sw], wkr[:, sw]], axis=1)
    pieces[PC["ckv"]:PC["ckv"] + 6] = _fm_pieces(ckv_cols)
    pieces[PC["cq"]:PC["cq"] + 4] = _fm_pieces(wcq)
    w_uq = np.asarray(inp["w_uq"], f)[0].reshape(512, 16, 192)
    w_ukv = np.asarray(inp["w_ukv"], f)[0].reshape(512, 16, 256)
    for h in range(16):
        rope = w_uq[:, h, 128:192]
        hw = np.concatenate([w_uq[:, h, 0:128], rope, rope[:, sw], w_ukv[:, h, 0:128], w_ukv[:, h, 128:256]], axis=1)
        pieces[PC["head"] + h] = hw.reshape(4, P, 512).transpose(1, 0, 2).reshape(P, PIECE)
    pieces[PC["zb"]:PC["zb"] + 16] = _fm_pieces(wzb)
    pieces[PC["u"]:PC["u"] + 16] = _fm_pieces(wu)
    pieces[PC["za"]:PC["za"] + 16] = _fm_pieces(wza)
    pieces[PC["v"]:PC["v"] + 16] = _tm_pieces(wv)
    wmkv = np.asarray(inp["w_mem_kv"], f)[0]
    pieces[PC["memk"]:PC["memk"] + 16] = _fm_pieces(wmkv[:, 0:2048])
    pieces[PC["memv"]:PC["memv"] + 16] = _tm_pieces(wmkv[:, 2048:4096])
    pieces[PC["qm"]:PC["qm"] + 16] = _fm_pieces(wqm)
    pieces[PC["zm"]:PC["zm"] + 16] = _fm_pieces(wzm)
    wg = np.asarray(inp["w_gate"], f)[0]
    wb = np.asarray(inp["w_branch"], f)[0]
    for n in range(3):
        pieces[PC["gate"] + 16 * n:PC["gate"] + 16 * n + 16] = _fm_pieces(wg[:, n * 2048:(n + 1) * 2048])
        pieces[PC["branch"] + 16 * n:PC["branch"] + 16 * n + 16] = _fm_pieces(wb[n])
    pieces[PC["out"]:PC["out"] + 16] = _tm_pieces(np.asarray(inp["w_out"], f)[0])

    small = np.zeros((P, SM_N), f)
    small[:, SM_GPRE:SM_GPRE + 16] = np.asarray(inp["g_pre"], f)[0].reshape(16, P).T
    small[:, SM_QNG:SM_QNG + 4] = np.asarray(inp["q_norm_g"], f)[0].reshape(4, P).T
    small[:, SM_KVG:SM_KVG + 4] = np.asarray(inp["kv_norm_g"], f)[0].reshape(4, P).T
    small[:, SM_MEMG:SM_MEMG + 16] = np.asarray(inp["mem_norm_g"], f)[0].reshape(16, P).T
    small[:, SM_BGATE:SM_BGATE + 48] = np.asarray(inp["b_gate"], f)[0].reshape(48, P).T
    inv_freq = (1.0 / (np.float32(10000.0) ** (np.arange(0, 64, 2, dtype=f) / np.float32(64)))).astype(f)
    pidx = np.arange(P)
    small[:, SM_CST + 0] = inv_freq[pidx % 32]
    small[:, SM_CST + 1] = np.where((pidx % 64) < 32, -1.0, 1.0)
    rows = np.concatenate([np.asarray(inp["a_ln_g"], f)[0], np.asarray(inp["a_ln_b"], f)[0],
                           np.asarray(inp["a_b_s"], f)[0].reshape(-1), np.asarray(inp["g_post"], f)[0]])[None]
    wsT = np.ascontiguousarray(np.asarray(inp["a_w_s"], f)[0].transpose(2, 0, 1)).reshape(P, 16 * P)
    ident = np.eye(P, dtype=f)
    xaT = np.ascontiguousarray(x.T.reshape(16, P, 16, 512).transpose(2, 1, 0, 3)).reshape(16, P, 8192)
    memT = np.ascontiguousarray(mem.T.reshape(16, P, 256).transpose(1, 0, 2)).reshape(P, 4096)
    shared = {"xaT": xaT, "pos_all": np.ascontiguousarray(pos.reshape(1, S)), "memT": memT, "wts": pieces,
              "small": small, "rows": np.ascontiguousarray(rows), "wsT": wsT, "ident": ident,
              "tri": (np.arange(P)[:, None] <= np.arange(P)[None, :]).astype(f)}
    per_core = []
    kk = np.arange(P)[:, None]
    qq = np.arange(P)[None, :]
    for c in range(NCORE):
        idx = np.concatenate([np.arange((c + 8 * j) * P, (c + 8 * j + 1) * P) for j in range(8)])
        xo = np.ascontiguousarray(x[idx])
        xoT = np.ascontiguousarray(xo.T.reshape(16, P, TOK).transpose(1, 0, 2)).reshape(P, 16 * TOK)
        mask = np.zeros((P, 8, P), f)
        for oo in range(8):
            if oo < c:
                mask[:, oo, :] = 1.0
            elif oo == c:
                mask[:, oo, :] = (kk <= qq).astype(f)
        d = dict(shared)
        d.update({"xo": xo, "xoT": xoT, "pos_own": np.ascontiguousarray(pos[0, idx].reshape(1, TOK)),
                  "mask": ((mask - 1.0) * 30000.0).astype(f).reshape(P, 8 * P)})
        per_core.append((idx, d))
    return per_core


class _Op:
    __slots__ = ("eng", "fn", "reads", "writes", "dma", "key", "ndma", "deps", "needed", "token", "waits", "idx")


class Sched:
    ENGS = ("pe", "act", "dve", "pool", "sp")

    def __init__(self):
        self.ops = []
        self.seen = set()
        self.bar_fn = None

    def barrier(self):
        regs = tuple(self.seen)
        self.add("dve", self.bar_fn, reads=regs, writes=regs + ("BAR",))

    def add(self, eng, fn, reads=(), writes=()):
        op = _Op()
        op.eng, op.fn, op.reads, op.writes = eng, fn, tuple(reads) + ("BAR",), tuple(writes)
        op.dma, op.key, op.ndma = False, None, 0
        self.seen.update(op.reads)
        self.seen.update(op.writes)
        op.idx = len(self.ops)
        self.ops.append(op)
        return op

    def dma(self, eng, fn, key, n, reads=(), writes=()):
        op = self.add(eng, fn, reads, writes)
        op.dma, op.key, op.ndma = True, key, n
        return op

    def finalize(self):
        last_w, readers = {}, {}
        for op in self.ops:
            deps = {}
            for r in op.reads:
                if r in last_w:
                    deps[last_w[r]] = "raw"
            for w in op.writes:
                if w in last_w:
                    deps.setdefault(last_w[w], "waw")
                for rd in readers.get(w, ()):
                    deps.setdefault(rd, "war")
            deps.pop(op.idx, None)
            op.deps = deps
            for r in op.reads:
                readers.setdefault(r, []).append(op.idx)
            for w in op.writes:
                last_w[w] = op.idx
                readers[w] = []
        for op in self.ops:
            keep = {}
            for i, kind in op.deps.items():
                src = self.ops[i]
                if (not src.dma) and (not op.dma) and src.eng == op.eng:
                    if op.eng == "pe":
                        continue
                keep[i] = kind
            op.deps = keep
        for op in self.ops:
            op.needed = False
        for op in self.ops:
            for i in op.deps:
                self.ops[i].needed = True
        cnt = {e: 0 for e in self.ENGS}
        dcnt = {}
        for op in self.ops:
            if op.dma:
                dcnt[op.key] = dcnt.get(op.key, 0) + 16 * op.ndma
                op.token = ("dma:" + op.key, dcnt[op.key])
            else:
                if op.needed:
                    cnt[op.eng] += 1
                op.token = ("eng:" + op.eng, cnt[op.eng])
        self.dma_keys = sorted(dcnt)
        waited = {e: {} for e in self.ENGS}
        for op in self.ops:
            need = {}
            for i in op.deps:
                s, v = self.ops[i].token
                if v > need.get(s, 0):
                    need[s] = v
            op.waits = []
            for s, v in need.items():
                if waited[op.eng].get(s, 0) < v:
                    waited[op.eng][s] = v
                    op.waits.append((s, v))

    def emit(self, nc, block, sems, final_waits):
        by_eng = {e: [op for op in self.ops if op.eng == e] for e in self.ENGS}

        def run(e, h):
            for op in by_eng[e]:
                for s, v in op.waits:
                    h.wait_ge(sems[s], v)
                if op.dma:
                    op.fn(h, sems["dma:" + op.key])
                else:
                    ins = op.fn(h)
                    if op.needed:
                        ins.then_inc(sems["eng:" + e], 1)
            if e == "sp":
                for s, v in final_waits:
                    h.wait_ge(sems[s], v)

        @block.tensor
        def _(h):
            run("pe", h)

        @block.scalar
        def _(h):
            run("act", h)

        @block.vector
        def _(h):
            run("dve", h)

        @block.gpsimd
        def _(h):
            run("pool", h)

        @block.sync
        def _(h):
            run("sp", h)


def build(stop_after=None, dbg=None):
    from contextlib import ExitStack
    nc = bass.Bass("TRN2", target_bir_lowering=False)
    dt = nc.dram_tensor
    xaT = dt("xaT", [16, P, 8192], F32, kind="ExternalInput").ap()
    pos_all = dt("pos_all", [1, S], I32, kind="ExternalInput").ap()
    memT = dt("memT", [P, 4096], F32, kind="ExternalInput").ap()
    wts = dt("wts", [NPIECE, P, PIECE], F32, kind="ExternalInput").ap()
    small_d = dt("small", [P, SM_N], F32, kind="ExternalInput").ap()
    rows_d = dt("rows", [1, 8192], F32, kind="ExternalInput").ap()
    wsT_d = dt("wsT", [P, 2048], F32, kind="ExternalInput").ap()
    ident_d = dt("ident", [P, P], F32, kind="ExternalInput").ap()
    xo_d = dt("xo", [TOK, D], F32, kind="ExternalInput").ap()
    xoT_d = dt("xoT", [P, 16 * TOK], F32, kind="ExternalInput").ap()
    pos_own = dt("pos_own", [1, TOK], I32, kind="ExternalInput").ap()
    mask_d = dt("mask", [P, 1024], F32, kind="ExternalInput").ap()
    tri_d = dt("tri", [P, P], F32, kind="ExternalInput").ap()
    y_d = dt("y", [TOK, D], F32, kind="ExternalOutput").ap()
    dbg_d = dt("dbg", [P, 8192], F32, kind="ExternalOutput").ap() if dbg else None

    es = ExitStack()
    sb = lambda name, shape, dtype: es.enter_context(nc.sbuf_tensor(name, shape, dtype))
    A1 = sb("A1", [P, 40960], BF16)
    A2 = sb("A2", [P, 16896], BF16)
    A3 = sb("A3", [P, 16384], BF16)
    A5 = sb("A5", [P, 14336], BF16)
    SCR = sb("SCR", [P, 8192], BF16)
    small = sb("smallc", [P, SM_N], F32)
    ones = sb("ones", [P, P], BF16)
    ident = sb("identb", [P, P], BF16)
    rstd_o = sb("rstd_o", [P, TOK], F32)
    Tq = sb("Tq", [P, TOK], F32)
    stat = sb("stat", [P, 128], F32)
    tri = sb("trib", [P, P], BF16)
    bart = sb("bart", [P, 8], F32)
    ps = es.enter_context(nc.psum_tensor("ps", [P, 8, 512], F32))

    def v2(arena, off, n, dtype=BF16):
        a = arena[:, off:off + n]
        return a if dtype == BF16 else a.bitcast(dtype)

    def v3(arena, off, a, b, dtype=BF16):
        n = a * b * (2 if dtype != BF16 else 1)
        return v2(arena, off, n, dtype).rearrange("p (a b) -> p a b", a=a)

    def pair(b):
        return ps[:, b:b + 2, :].rearrange("p b c -> p (b c)")

    sc = Sched()
    sc.bar_fn = lambda h: h.memset(bart[:], 0.0)
    cst = small[:, SM_CST:SM_CST + 8]
    invf = cst[:, 0:1]
    sgn = cst[:, 1:2]

    def load_piece(idx, dst_ap, region, key):
        def fn(h, sem, idx=idx, dst_ap=dst_ap):
            h.dma_start(out=dst_ap, in_=wts[idx]).then_inc(sem, 16)
        sc.dma("pool", fn, key, 1, writes=(region,))

    class WStream:
        def __init__(self, name, bufs, plist):
            self.name, self.bufs, self.plist = name, bufs, plist
            self.issued = 0
            self.used = 0

        def _issue_upto(self, n):
            while self.issued < min(n, len(self.plist)):
                i = self.issued
                b = i % len(self.bufs)
                load_piece(self.plist[i], self.bufs[b], (self.name, b), "%s%d" % (self.name, b))
                self.issued += 1

        def next(self):
            i = self.used
            self._issue_upto(i + len(self.bufs))
            self.used += 1
            b = i % len(self.bufs)
            return self.bufs[b], (self.name, b)

        def prefetch(self):
            self._issue_upto(self.used + len(self.bufs))

    def rsqrt_bc(src_ps_ap, dst_ap, n_feat, src_regions, dst_region, tmp_ap, tmp_region):
        sc.add("act", lambda h: h.activation(out=tmp_ap, in_=src_ps_ap, func=AF.Sqrt, bias=EPS, scale=1.0 / n_feat),
               reads=src_regions, writes=(tmp_region,))
        sc.add("dve", lambda h: h.reciprocal(out=dst_ap, in_=tmp_ap), reads=(tmp_region,), writes=(dst_region,))

    def rope_tables(posi, pos_src_ap, n, scratch, names, outs):
        a, kf, m = scratch
        ra, rk, rm, rp = names
        sc.dma("sp", lambda h, sem: h.dma_start(out=posi, in_=pos_src_ap.to_broadcast([P, n])).then_inc(sem, 16),
               rp, 1, writes=(rp,))
        sc.add("dve", lambda h: h.tensor_scalar(out=a, in0=posi, scalar1=invf, scalar2=None, op0=ALU.mult), reads=(rp, "small"), writes=(ra,))
        sc.add("dve", lambda h: h.tensor_scalar(out=posi, in0=a, scalar1=1.0 / TWO_PI, scalar2=None, op0=ALU.mult), reads=(ra,), writes=(rp,))
        sc.add("dve", lambda h: h.tensor_copy(out=kf, in_=posi), reads=(rp,), writes=(rk,))
        sc.add("dve", lambda h: h.scalar_tensor_tensor(out=a, in0=kf, scalar=-C1, in1=a, op0=ALU.mult, op1=ALU.add), reads=(rk, ra), writes=(ra,))
        sc.add("dve", lambda h: h.scalar_tensor_tensor(out=a, in0=kf, scalar=-C2, in1=a, op0=ALU.mult, op1=ALU.add), reads=(rk, ra), writes=(ra,))
        sc.add("dve", lambda h: h.tensor_scalar(out=m, in0=a, scalar1=float(np.pi), scalar2=-TWO_PI, op0=ALU.is_gt, op1=ALU.mult), reads=(ra,), writes=(rm,))
        sc.add("dve", lambda h: h.tensor_tensor(out=a, in0=a, in1=m, op=ALU.add), reads=(ra, rm), writes=(ra,))
        sc.add("dve", lambda h: h.tensor_scalar(out=m, in0=a, scalar1=float(-np.pi), scalar2=TWO_PI, op0=ALU.is_lt, op1=ALU.mult), reads=(ra,), writes=(rm,))
        sc.add("dve", lambda h: h.tensor_tensor(out=a, in0=a, in1=m, op=ALU.add), reads=(ra, rm), writes=(ra,))
        sc.add("dve", lambda h: h.scalar_tensor_tensor(out=kf, in0=a, scalar=-1.0, in1=a, op0=ALU.mult, op1=ALU.min), reads=(ra,), writes=(rk,))
        for kind, out_ap, psl, region in outs:
            if kind == "cos":
                sc.add("act", lambda h, out_ap=out_ap, psl=psl: h.activation(out=out_ap, in_=kf[psl], func=AF.Sin, bias=halfpi[psl], scale=1.0),
                       reads=(rk, "halfpi"), writes=(region,))
            else:
                sc.add("act", lambda h, out_ap=out_ap, psl=psl: h.activation(out=out_ap, in_=a[psl], func=AF.Sin, scale=sgn[psl]),
                       reads=(ra, "small"), writes=(region,))

    halfpi = stat[:, 127:128]

    def setup_fn(h, sem):
        h.dma_start(out=small[:], in_=small_d[:, :]).then_inc(sem, 16)
    sc.dma("sp", setup_fn, "small", 1, writes=("small",))
    sc.dma("pool", lambda h, sem: h.dma_start(out=ident[:], in_=ident_d[:, :]).then_inc(sem, 16), "ident", 1, writes=("ident",))
    sc.add("dve", lambda h: h.memset(ones[:], 1.0), writes=("ones",))
    sc.add("dve", lambda h: h.memset(halfpi, float(np.pi / 2)), writes=("halfpi",))
    SMALL = ("small", "halfpi")

    dbg_state = {"done": False}

    def finish_dbg(view_ap, region, nfree, is_f32):
        def fn(h, sem):
            h.dma_start(out=dbg_d[:, 0:nfree], in_=view_ap).then_inc(sem, 16)
        sc.dma("pool", fn, "dbgout", 1, reads=(region,) if not isinstance(region, list) else tuple(region))
        dbg_state["done"] = True

    ckvn = v3(A1, 0, 4, 8192)
    kpe2 = v2(A1, 32768, 8192)
    xa = [v3(A3, 0, 16, 512), v3(A3, 8192, 16, 512)]
    sq0 = v3(A2, 0, 16, 512)
    ckvf = v3(A2, 8192, 6, 512, F32)
    sq2 = v3(A2, 14336, 4, 512)
    wckv = [v3(A5, i * PIECE, 16, 128) for i in range(6)]
    f32s = lambda i: v2(SCR, i * 1024, 1024, F32)
    cosK, ssK, t_a, t_k, t_m, rstd_bc, rt_tmp = [f32s(i) for i in range(7)]
    posi0 = v2(SCR, 7 * 1024, 1024, I32)

    for i in range(6):
        load_piece(PC["ckv"] + i, v2(A5, i * PIECE, PIECE), ("wckv", i), "wckv%d" % i)
    for i in range(6):
        for k in range(16):
            sc.add("dve", lambda h, i=i, k=k: h.tensor_scalar(out=wckv[i][:, k, :], in0=wckv[i][:, k, :],
                                                            scalar1=small[:, SM_GPRE + k:SM_GPRE + k + 1], scalar2=None, op0=ALU.mult),
                   reads=(("wckv", i), "small"), writes=(("wckv", i),))

    def load_xa(n):
        b = n % 2
        for hf in range(2):
            def fn(h, sem, n=n, b=b, hf=hf):
                h.dma_start(out=v2(A3, b * 8192 + hf * 4096, 4096), in_=xaT[n][:, hf * 4096:(hf + 1) * 4096]).then_inc(sem, 16)
            sc.dma("pool", fn, "xa%d_%d" % (b, hf), 1, writes=(("xa", b, hf),))

    load_xa(0)
    NT0 = 16

    def do_rope0(n):
        tsl_ = slice(n * 512, (n + 1) * 512)
        rope_tables(posi0, pos_all[0:1, tsl_], 512, (t_a, t_k, t_m), ("t_a", "t_k", "t_m", "posi0"),
                    [("cos", cosK, slice(0, P), "cosK"), ("ss", ssK, slice(0, P), "ssK")])

    def do_sq0(n):
        b_ = n % 2
        sc.add("act", lambda h, b_=b_: h.activation(out=sq0[:, 0:8, :], in_=xa[b_][:, 0:8, :], func=AF.Square), reads=(("xa", b_, 0),), writes=(("sq0", 0),))

    def do_sq0b(n):
        b_ = n % 2
        sc.add("dve", lambda h, b_=b_: h.tensor_tensor(out=sq0[:, 8:16, :], in0=xa[b_][:, 8:16, :], in1=xa[b_][:, 8:16, :], op=ALU.mult),
               reads=(("xa", b_, 1),), writes=(("sq0", 1),))

    do_rope0(0)
    do_sq0(0)
    do_sq0b(0)
    for n in range(NT0):
        b = n % 2
        if n + 1 < NT0:
            load_xa(n + 1)
        tsl = slice(n * 512, (n + 1) * 512)

        def mm_ss(h, b=b):
            ins = None
            for k in range(16):
                ins = h.matmul(ps[:, 6, :], ones[:], sq0[:, k, :], start=(k == 0), stop=(k == 15))
            return ins
        sc.add("pe", mm_ss, reads=(("sq0", 0), ("sq0", 1), "ones"), writes=("ps6",))
        rsqrt_bc(ps[:, 6, :], rstd_bc, float(D), ("ps6",), "rstd_bc", rt_tmp, "rt_tmp")
        for c in range(6):
            bank = c % 4

            def mm_c(h, c=c, b=b, bank=bank):
                ins = None
                for k in range(16):
                    ins = h.matmul(ps[:, bank, :], wckv[c][:, k, :], xa[b][:, k, :], start=(k == 0), stop=(k == 15))
                return ins
            sc.add("pe", mm_c, reads=(("wckv", c), ("xa", b, 0), ("xa", b, 1)), writes=("ps%d" % bank,))
            sc.add("dve", lambda h, c=c, bank=bank: h.tensor_tensor(out=ckvf[:, c, :], in0=ps[:, bank, :], in1=rstd_bc, op=ALU.mult),
                   reads=("ps%d" % bank, "rstd_bc"), writes=(("ckvf", c),))
        if n + 1 < NT0:
            do_sq0(n + 1)
        sc.add("act", lambda h: h.activation(out=sq2, in_=ckvf[:, 0:4, :], func=AF.Square),
               reads=tuple(("ckvf", c) for c in range(4)), writes=("sq2",))

        def mm_ss2(h):
            ins = None
            for k in range(4):
                ins = h.matmul(ps[:, 7, :], ones[:], sq2[:, k, :], start=(k == 0), stop=(k == 3))
            return ins
        sc.add("pe", mm_ss2, reads=("sq2", "ones"), writes=("ps7",))
        rsqrt_bc(ps[:, 7, :], rt_tmp, 512.0, ("ps7",), "rt_tmp", t_m, "t_m")
        for k in range(4):
            sc.add("dve", lambda h, k=k, tsl=tsl: h.scalar_tensor_tensor(out=ckvn[:, k, tsl], in0=ckvf[:, k, :],
                                                                      scalar=small[:, SM_KVG + k:SM_KVG + k + 1], in1=rt_tmp,
                                                                      op0=ALU.mult, op1=ALU.mult),
                   reads=(("ckvf", k), "rt_tmp", "small"), writes=(("ckvn", n),))
        sc.add("dve", lambda h: h.tensor_tensor(out=t_a, in0=ckvf[:, 4, :], in1=cosK, op=ALU.mult), reads=(("ckvf", 4), "cosK"), writes=("t_a",))
        sc.add("dve", lambda h: h.tensor_tensor(out=t_k, in0=ckvf[:, 5, :], in1=ssK, op=ALU.mult), reads=(("ckvf", 5), "ssK"), writes=("t_k",))
        sc.add("dve", lambda h, tsl=tsl: h.tensor_tensor(out=kpe2[:, tsl], in0=t_a, in1=t_k, op=ALU.add), reads=("t_a", "t_k"), writes=(("kpe2", n),))
        if n + 1 < NT0:
            do_sq0b(n + 1)
            do_rope0(n + 1)

    def dump(seg, view_ap, regions):
        if not dbg:
            return
        def fn(h, sem, seg=seg, view_ap=view_ap):
            h.dma_start(out=dbg_d[:, seg * 1024:(seg + 1) * 1024], in_=view_ap).then_inc(sem, 16)
        sc.dma("pool", fn, "dbg%d" % seg, 1, reads=tuple(regions))
        dbg_segs.append(seg)

    dbg_segs = []
    dump(0, ckvn[:, 0, 0:1024], [("ckvn", 0), ("ckvn", 1)])

    def mm32(out_pair_b, w3, act3, h):
        ins = None
        for k in range(16):
            for half in range(2):
                ins = h.matmul(ps[:, out_pair_b + half, :], w3[:, k, :], act3[:, k, half * 512:(half + 1) * 512],
                               start=(k == 0), stop=(k == 15))
        return ins

    def pregs(b, n=2):
        return tuple("ps%d" % (b + i) for i in range(n))

    sc.barrier()
    hT = v3(A2, 0, 16, 1024)
    HT = tuple(("hT", k) for k in range(16))
    sqo = v3(A3, 0, 16, 1024)
    cqf = v3(A3, 0, 4, 1024, F32)
    sq2o = v3(A3, 8192, 4, 1024)
    cqn = v3(A5, 0, 4, 1024)
    o_slots = [v2(SCR, i * 2048, 2048, F32) for i in range(4)]
    posi1 = v2(SCR, 0, 2048, I32)
    o_a, o_k, o_m = o_slots[1], o_slots[2], o_slots[3]

    def load_hT():
        def fn(h, sem):
            for q in range(4):
                h.dma_start(out=v2(A2, q * 4096, 4096), in_=xoT_d[:, q * 4096:(q + 1) * 4096]).then_inc(sem, 16)
        sc.dma("pool", fn, "hTld", 4, writes=HT)

    def scale_hT():
        for k in range(16):
            sc.add("dve", lambda h, k=k: h.scalar_tensor_tensor(out=hT[:, k, :], in0=hT[:, k, :], scalar=small[:, SM_GPRE + k:SM_GPRE + k + 1],
                                                              in1=rstd_o[:], op0=ALU.mult, op1=ALU.mult),
                   reads=(("hT", k), "rstd_o", "small"), writes=(("hT", k),))

    load_hT()
    sc.add("dve", lambda h: h.tensor_tensor(out=sqo, in0=hT, in1=hT, op=ALU.mult), reads=HT, writes=("sqo",))

    def mm_sso(h):
        ins = None
        for k in range(16):
            for half in range(2):
                ins = h.matmul(ps[:, half, :], ones[:], sqo[:, k, half * 512:(half + 1) * 512], start=(k == 0), stop=(k == 15))
        return ins
    sc.add("pe", mm_sso, reads=("sqo", "ones"), writes=pregs(0))
    rsqrt_bc(pair(0), rstd_o[:], float(D), pregs(0), "rstd_o", o_m, "o_m")
    scale_hT()
    rope_tables(posi1, pos_own[0:1, :], 1024, (o_a, o_k, o_m), ("o_a", "o_k", "o_m", "posi1"),
                [("cos", Tq[0:64, :], slice(0, 64), "Tq0"), ("ss", Tq[64:128, :], slice(64, 128), "Tq1")])
    ws1 = WStream("w1", [v2(A5, 4096 + i * PIECE, PIECE) for i in range(5)], [PC["cq"] + i for i in range(4)])
    for c in range(4):
        w, wr = ws1.next()
        w3 = w.rearrange("p (k c) -> p k c", k=16)
        b = (c % 2) * 2
        sc.add("pe", lambda h, b=b, w3=w3: mm32(b, w3, hT, h), reads=(wr,) + HT, writes=pregs(b))
        sc.add("act", lambda h, b=b, c=c: h.activation(out=cqf[:, c, :], in_=pair(b), func=AF.Copy), reads=pregs(b), writes=(("cqf", c),))
    CQF = tuple(("cqf", c) for c in range(4))
    sc.add("act", lambda h: h.activation(out=sq2o, in_=cqf, func=AF.Square), reads=CQF, writes=("sq2o",))

    def mm_ssq(h):
        ins = None
        for k in range(4):
            for half in range(2):
                ins = h.matmul(ps[:, 4 + half, :], ones[:], sq2o[:, k, half * 512:(half + 1) * 512], start=(k == 0), stop=(k == 3))
        return ins
    sc.add("pe", mm_ssq, reads=("sq2o", "ones"), writes=pregs(4))
    rsqrt_bc(pair(4), o_a, 512.0, pregs(4), "o_a", o_k, "o_k")
    for k in range(4):
        sc.add("dve", lambda h, k=k: h.scalar_tensor_tensor(out=cqn[:, k, :], in0=cqf[:, k, :], scalar=small[:, SM_QNG + k:SM_QNG + k + 1],
                                                          in1=o_a, op0=ALU.mult, op1=ALU.mult),
               reads=(("cqf", k), "o_a", "small"), writes=(("cqn", k),))
    CQN = tuple(("cqn", k) for k in range(4))
    dump(1, hT[:, 0, :], [("hT", 0)])
    dump(2, cqn[:, 0, :], [("cqn", 0)])

    sc.barrier()
    qn = v2(A5, 4096, 1024)
    qr = v2(A5, 5120, 1024)
    pT = [v2(A5, 6144 + i * 1024, 1024) for i in range(3)]
    maskb = v3(A5, 9216, 8, 128)
    kT = v2(A2, 0, 8192)
    vtok = v3(A2, 8192, 64, 128)
    yb = v3(A3, 0, 16, 1024)
    accD = v2(SCR, 0, 2048, F32)
    accP = v2(SCR, 2048, 2048, F32)
    hi_b = v2(SCR, 4096, 1024)
    lo_b = v2(SCR, 5120, 1024)
    rsb = v2(SCR, 6144, 2048, F32)
    QSC = float(192.0 ** -0.5)
    sc.dma("pool", lambda h, sem: h.dma_start(out=v2(A5, 9216, 1024), in_=mask_d[:, :]).then_inc(sem, 16), "mask", 1, writes=("maskb",))
    sc.dma("pool", lambda h, sem: h.dma_start(out=tri[:], in_=tri_d[:, :]).then_inc(sem, 16), "tri", 1, writes=("tri",))
    wsh = WStream("wh", [v2(A5, 10240 + i * PIECE, PIECE) for i in range(2)], [PC["head"] + h for h in range(16)])

    def oacc(j):
        return ps[:, 4 + j // 3, (j % 3) * 129:(j % 3) * 129 + 129]

    def finalize_head(hd):
        sc.add("dve", lambda h: h.tensor_copy(out=hi_b, in_=accD), reads=("accD",), writes=("hi_b",))
        sc.add("dve", lambda h: h.tensor_tensor(out=accD, in0=accD, in1=hi_b, op=ALU.subtract), reads=("accD", "hi_b"), writes=("accD",))
        sc.add("dve", lambda h: h.tensor_copy(out=lo_b, in_=accD), reads=("accD",), writes=("lo_b",))

        def mm_sum(h):
            ins = None
            for half in range(2):
                h.matmul(ps[:, half, :], ones[:], hi_b[:, half * 512:(half + 1) * 512], start=True, stop=False)
                ins = h.matmul(ps[:, half, :], ones[:], lo_b[:, half * 512:(half + 1) * 512], start=False, stop=True)
            return ins
        sc.add("pe", mm_sum, reads=("hi_b", "lo_b", "ones"), writes=pregs(0, 2))
        sc.add("dve", lambda h: h.reciprocal(out=rsb, in_=pair(0)), reads=pregs(0, 2), writes=("rsb",))
        sc.add("dve", lambda h, hd=hd: h.tensor_tensor(out=yb[:, hd, :], in0=pair(6), in1=rsb, op=ALU.mult), reads=pregs(6, 2) + ("rsb",), writes=(("yb", hd),))

    NHEAD = 16
    for hd in range(NHEAD):
        w, wr = wsh.next()
        wh = w.rearrange("p (k c) -> p k c", k=4)

        def mm_q(h, wh=wh, lo=0, b=0):
            ins = None
            for k in range(4):
                for half in range(2):
                    ins = h.matmul(ps[:, b + half, :], wh[:, k, lo:lo + 128], cqn[:, k, half * 512:(half + 1) * 512], start=(k == 0), stop=(k == 3))
            return ins
        sc.add("pe", lambda h, wh=wh: mm_q(h, wh, 0, 0), reads=(wr,) + CQN, writes=pregs(0))
        sc.add("act", lambda h: h.activation(out=qn, in_=pair(0), func=AF.Copy, scale=QSC), reads=pregs(0), writes=("qn",))
        sc.add("pe", lambda h, wh=wh: mm_q(h, wh, 128, 2), reads=(wr,) + CQN, writes=pregs(2))
        sc.add("dve", lambda h: h.scalar_tensor_tensor(out=qr, in0=pair(2), scalar=QSC, in1=Tq[:], op0=ALU.mult, op1=ALU.mult),
               reads=pregs(2) + ("Tq0", "Tq1"), writes=("qr",))
        for t in range(16):
            bank = t % 4

            def mm_k(h, wh=wh, t=t, bank=bank):
                ins = None
                for k in range(4):
                    ins = h.matmul(ps[:, bank, :], wh[:, k, 256:384], ckvn[:, k, t * 512:(t + 1) * 512], start=(k == 0), stop=(k == 3))
                return ins
            sc.add("pe", mm_k, reads=(wr, ("ckvn", t)), writes=pregs(bank, 1))
            sc.add("act", lambda h, t=t, bank=bank: h.activation(out=kT[:, t * 512:(t + 1) * 512], in_=ps[:, bank, :], func=AF.Copy),
                   reads=pregs(bank, 1), writes=(("kT", t),))
        for g4 in range(16):
            bank = g4 % 4

            def mm_v(h, wh=wh, g4=g4, bank=bank):
                ins = None
                for bi in range(4):
                    blk = g4 * 4 + bi
                    for k in range(4):
                        ins = h.matmul(ps[:, bank, bi * 128:(bi + 1) * 128], ckvn[:, k, blk * 128:(blk + 1) * 128], wh[:, k, 384:512],
                                       start=(k == 0), stop=(k == 3))
                return ins
            sc.add("pe", mm_v, reads=(wr, ("ckvn", g4)), writes=pregs(bank, 1))
            sc.add("act" if g4 % 2 == 0 else "dve", (lambda h, g4=g4, bank=bank: h.activation(out=vtok[:, g4 * 4:(g4 + 1) * 4, :],
                                                                   in_=ps[:, bank, :].rearrange("p (a b) -> p a b", a=4), func=AF.Copy)) if g4 % 2 == 0 else
                   (lambda h, g4=g4, bank=bank: h.tensor_copy(out=vtok[:, g4 * 4:(g4 + 1) * 4, :], in_=ps[:, bank, :].rearrange("p (a b) -> p a b", a=4))),
                   reads=pregs(bank, 1), writes=(("vaug", g4),))

        def qk(kb):
            G = kb // 8
            pb = (kb % 3) * 2
            if G < 4:
                rngs = [(pb, G * 128, 512, 0), (pb + 1, 512, 1024, 512)]
            else:
                rngs = [(pb + 1, G * 128, 1024, 512)]

            def fn(h, kb=kb, rngs=rngs):
                ins = None
                for ri, (bank, c0, c1, off) in enumerate(rngs):
                    h.matmul(ps[:, bank, c0 - off:c1 - off], kT[:, kb * 128:(kb + 1) * 128], qn[:, c0:c1], start=True, stop=False)
                    if ri == 0:
                        h.matmul(ps[:, bank, c0 - off:c0 - off + 128], ident[:], maskb[:, kb % 8, :], start=False, stop=False)
                    ins = h.matmul(ps[:, bank, c0 - off:c1 - off], kpe2[:, kb * 128:(kb + 1) * 128], qr[:, c0:c1], start=False, stop=True)
                return ins
            sc.add("pe", fn, reads=(("kT", kb // 4), ("kpe2", kb // 4), "qn", "qr", "maskb", "ident"), writes=pregs(pb))

        if hd > 0:
            finalize_head(hd - 1)
        sc.add("dve", lambda h: h.memset(accD, 0.0), writes=("accD",))
        qk(0)
        qk(1)
        for kb in range(64):
            G = kb // 8
            pb = (kb % 3) * 2
            pt = pT[kb % 3]
            ptr = ("pT", kb % 3)
            if kb + 2 < 64:
                qk(kb + 2)
            sc.add("act", lambda h, G=G, pb=pb, pt=pt: h.activation(out=pt[:, G * 128:1024], in_=pair(pb)[:, G * 128:1024], func=AF.Exp),
                   reads=pregs(pb), writes=(ptr,))
            if G < 4:
                prng = [(0, G * 128, 512, 0), (1, 512, 1024, 512)]
            else:
                prng = [(1, G * 128, 1024, 512)]

            def pv(h, kb=kb, pt=pt, prng=prng):
                ins = None
                for half, c0, c1, off in prng:
                    ins = h.matmul(ps[:, 6 + half, c0 - off:c1 - off], vtok[:, kb, :], pt[:, c0:c1], start=(kb == 0),
                                   stop=(kb == (31 if half == 0 else 63)))
                return ins
            sc.add("pe", pv, reads=(ptr, ("vaug", kb // 4)), writes=pregs(6, 2))
            sc.add("dve", lambda h, G=G, pt=pt: h.tensor_tensor(out=accD[:, G * 128:1024], in0=accD[:, G * 128:1024], in1=pt[:, G * 128:1024], op=ALU.add),
                   reads=(ptr, "accD"), writes=("accD",))
    finalize_head(NHEAD - 1)
    dump(3, yb[:, 0, :], [("yb", 0)])
    dump(4, yb[:, 5, :], [("yb", 5)])

    sc.barrier()
    xf32 = v3(A1, 0, 16, 1024, F32)

    def fn_xf(h, sem):
        for q in range(4):
            h.dma_start(out=v2(A1, q * 8192, 8192, F32), in_=xoT_d[:, q * 4096:(q + 1) * 4096]).then_inc(sem, 16)
    sc.dma("sp", fn_xf, "xf32", 4, writes=("xf32",))
    for k in range(16):
        sc.add("dve", lambda h, k=k: h.scalar_tensor_tensor(out=hT[:, k, :], in0=xf32[:, k, :], scalar=small[:, SM_GPRE + k:SM_GPRE + k + 1],
                                                          in1=rstd_o[:], op0=ALU.mult, op1=ALU.mult),
               reads=("xf32", "rstd_o", "small"), writes=(("hT", k),))
    sc.barrier()
    zt = [v2(A5, 10240 + i * 1024, 1024) for i in range(2)]
    ws3 = WStream("w3", [v2(A5, i * PIECE, PIECE) for i in range(4)],
                  [PC["zb"] + c for c in range(16)] + [PC["u"] + c for c in range(16)] + [PC["za"] + c for c in range(16)]
                  + [PC["v"] + i for i in range(16)])
    pcnt = [0]

    def next_pair():
        b = (pcnt[0] % 4) * 2
        pcnt[0] += 1
        return b

    def proj_chunk(ws, act3, act_regs):
        w, wr = ws.next()
        w3 = w.rearrange("p (k c) -> p k c", k=16)
        b = next_pair()
        sc.add("pe", lambda h, b=b, w3=w3: mm32(b, w3, act3, h), reads=(wr,) + tuple(act_regs), writes=pregs(b))
        return b

    for c in range(16):
        b = proj_chunk(ws3, hT, HT)
        z = zt[c % 2]
        sc.add("act", lambda h, b=b, z=z: h.activation(out=z, in_=pair(b), func=AF.Silu), reads=pregs(b), writes=(("zt", c % 2),))
        sc.add("dve", lambda h, c=c, z=z: h.tensor_tensor(out=yb[:, c, :], in0=yb[:, c, :], in1=z, op=ALU.mult),
               reads=(("yb", c), ("zt", c % 2)), writes=(("yb", c),))
    ya = v3(A1, 0, 16, 1024)
    vg = v3(A1, 16384, 8, 2048)
    lnG = v2(A1, 32768, 4096, F32)
    lnB = v2(A1, 36864, 4096, F32)
    bsb = v3(SCR, 0, 16, 128, F32)
    wsTb = v3(SCR, 4096, 16, 128)
    vlnb = [v2(SCR, 6144, 2048), v2(A5, 8192, 2048)]
    t32 = v2(A5, 12288, 2048, F32)

    def ld_rows(dst, off, key):
        sc.dma("sp", lambda h, sem: h.dma_start(out=dst, in_=rows_d[0:1, off:off + 2048].to_broadcast([P, 2048])).then_inc(sem, 16),
               key, 1, writes=(key,))
    ld_rows(lnG, ROW_LNG, "lnG")
    ld_rows(lnB, ROW_LNB, "lnB")
    ld_rows(v2(SCR, 0, 4096, F32), ROW_BS, "bsb")
    sc.dma("pool", lambda h, sem: h.dma_start(out=v2(SCR, 4096, 2048), in_=wsT_d[:, :]).then_inc(sem, 16), "wsTb", 1, writes=("wsTb",))
    for g in range(16):
        sc.add("dve", lambda h, g=g: h.tensor_tensor(out=wsTb[:, g, :], in0=wsTb[:, g, :], in1=tri[:], op=ALU.mult),
               reads=("wsTb", "tri"), writes=("wsTb",))
    for c in range(16):
        b = proj_chunk(ws3, hT, HT)
        sc.add("act", lambda h, b=b, c=c: h.activation(out=ya[:, c, :], in_=pair(b), func=AF.Gelu_apprx_tanh), reads=pregs(b), writes=(("ya", c),))
    for c in range(16):
        b = proj_chunk(ws3, hT, HT)
        z = zt[c % 2]
        sc.add("act", lambda h, b=b, z=z: h.activation(out=z, in_=pair(b), func=AF.Silu), reads=pregs(b), writes=(("zt", c % 2),))
        sc.add("dve", lambda h, c=c, z=z: h.tensor_tensor(out=ya[:, c, :], in0=ya[:, c, :], in1=z, op=ALU.mult),
               reads=(("ya", c), ("zt", c % 2)), writes=(("ya", c),))
    s1 = stat[:, 0:32]
    s2 = stat[:, 32:64]
    mean = stat[:, 64:72]
    var = stat[:, 72:80]
    ex2 = stat[:, 80:88]
    sc.add("dve", lambda h: h.memset(stat[:, 0:88], 0.0), writes=("s1", "s2", "mean", "var", "ex2"))
    for grp in range(4):
        for kq in range(4):
            w, wr = ws3.next()
            w3 = w.rearrange("p (k c) -> p k c", k=4)

            def mm_vt(h, w3=w3, kq=kq):
                ins = None
                for j in range(8):
                    for kk in range(4):
                        ins = h.matmul(ps[:, j, :], hT[:, kq * 4 + kk, j * 128:(j + 1) * 128], w3[:, kk, :],
                                       start=(kq == 0 and kk == 0), stop=(kq == 3 and kk == 3))
                return ins
            sc.add("pe", mm_vt, reads=(wr,) + HT, writes=pregs(0, 8))
        for j in range(8):
            sc.add("act", lambda h, j=j, grp=grp: h.activation(out=vg[:, j, grp * 512:(grp + 1) * 512], in_=ps[:, j, :], func=AF.Gelu_apprx_tanh,
                                                             accum_out=s1[:, j * 4 + grp:j * 4 + grp + 1]),
                   reads=("ps%d" % j, "s1"), writes=(("vg", j, grp), "s1"))
            sc.add("act", lambda h, j=j, grp=grp: h.activation(out=vlnb[0][:, 0:512], in_=vg[:, j, grp * 512:(grp + 1) * 512], func=AF.Square,
                                                             accum_out=s2[:, j * 4 + grp:j * 4 + grp + 1]),
                   reads=(("vg", j, grp), "s2"), writes=(("vln", 0), "s2"))
    sc.add("dve", lambda h: h.reduce_sum(out=mean, in_=s1.rearrange("p (j g) -> p j g", g=4), axis=AX.X), reads=("s1",), writes=("mean",))
    sc.add("dve", lambda h: h.reduce_sum(out=ex2, in_=s2.rearrange("p (j g) -> p j g", g=4), axis=AX.X), reads=("s2",), writes=("ex2",))
    sc.add("dve", lambda h: h.tensor_scalar(out=mean, in0=mean, scalar1=1.0 / 2048, scalar2=None, op0=ALU.mult), reads=("mean",), writes=("mean",))
    sc.add("dve", lambda h: h.tensor_tensor(out=var, in0=mean, in1=mean, op=ALU.mult), reads=("mean",), writes=("var",))
    sc.add("dve", lambda h: h.scalar_tensor_tensor(out=var, in0=ex2, scalar=1.0 / 2048, in1=var, op0=ALU.mult, op1=ALU.subtract),
           reads=("ex2", "var"), writes=("var",))
    sc.add("act", lambda h: h.activation(out=var, in_=var, func=AF.Sqrt, bias=EPS, scale=1.0), reads=("var",), writes=("var",))
    sc.add("dve", lambda h: h.reciprocal(out=var, in_=var), reads=("var",), writes=("var",))
    VGJ = lambda j: tuple(("vg", j, g) for g in range(4))

    def do_ln(j):
        vl = vlnb[j % 2]
        vr = ("vln", j % 2)
        sc.add("dve", lambda h, j=j, vl=vl: h.scalar_tensor_tensor(out=vl, in0=vg[:, j, :], scalar=mean[:, j:j + 1], in1=lnG, op0=ALU.subtract, op1=ALU.mult),
               reads=VGJ(j) + ("mean", "lnG"), writes=(vr,))
        sc.add("dve", lambda h, j=j, vl=vl: h.scalar_tensor_tensor(out=vl, in0=vl, scalar=var[:, j:j + 1], in1=lnB, op0=ALU.mult, op1=ALU.add),
               reads=(vr, "var", "lnB"), writes=(vr,))

    do_ln(0)
    for j in range(8):
        if j + 1 < 8:
            do_ln(j + 1)
        vl = vlnb[j % 2]
        vr = ("vln", j % 2)
        for gq in range(4):
            def mm_sp(h, gq=gq, vl=vl):
                ins = None
                for gi in range(4):
                    g = gq * 4 + gi
                    ins = h.matmul(ps[:, gq, gi * 128:(gi + 1) * 128], vl[:, g * 128:(g + 1) * 128], wsTb[:, g, :], start=True, stop=True)
                return ins
            sc.add("pe", mm_sp, reads=(vr, "wsTb"), writes=pregs(gq, 1))
            t32v = t32[:, (gq % 2) * 512:(gq % 2) * 512 + 512].rearrange("p (a b) -> p a b", a=4)
            tr_ = ("t32", gq % 2)
            sc.add("dve", lambda h, gq=gq, t32v=t32v: h.tensor_tensor(out=t32v, in0=ps[:, gq, :].rearrange("p (a b) -> p a b", a=4),
                                                                    in1=bsb[:, gq * 4:(gq + 1) * 4, :], op=ALU.add),
                   reads=pregs(gq, 1) + ("bsb",), writes=(tr_,))
            sc.add("dve", lambda h, gq=gq, j=j, t32v=t32v: h.tensor_tensor(out=ya[:, gq * 4:(gq + 1) * 4, j * 128:(j + 1) * 128],
                                                                         in0=ya[:, gq * 4:(gq + 1) * 4, j * 128:(j + 1) * 128], in1=t32v, op=ALU.mult),
                   reads=(tr_,) + tuple(("ya", gq * 4 + i) for i in range(4)), writes=tuple(("ya", gq * 4 + i) for i in range(4)))
    dump(5, ya[:, 0, :], [("ya", 0)])

    sc.barrier()
    memn = v3(A1, 32768, 16, 256)
    kmT = v3(A1, 36864, 16, 256)
    ym = v3(A1, 16384, 16, 1024)
    vm = v3(SCR, 0, 2, 2048)
    msq = v3(SCR, 4096, 16, 256)
    qm = v3(SCR, 4096, 4, 1024)
    ztm = v2(A5, 8192, 1024)
    pm = v3(A5, 9216, 2, 1024)
    rs = v2(A5, 11264, 2048, F32)
    rstd_m = v2(A5, 13312, 512, F32)
    tmpm = v2(A5, 13824, 512, F32)
    MSC = float(512.0 ** -0.5)
    plist = [PC["memk"] + c for c in range(16)] + [PC["memv"] + i for i in range(16)]
    for hm in range(4):
        plist += [PC["qm"] + hm * 4 + dc for dc in range(4)]
        plist += [PC["zm"] + hm * 4 + dc for dc in range(4)]
    wsm = WStream("wm", [v2(A5, i * PIECE, PIECE) for i in range(4)], plist)
    sc.dma("pool", lambda h, sem: h.dma_start(out=v2(A1, 32768, 4096), in_=memT[:, :]).then_inc(sem, 16), "memn", 1, writes=("memn",))
    sc.add("dve", lambda h: h.tensor_tensor(out=msq, in0=memn, in1=memn, op=ALU.mult), reads=("memn",), writes=("msq",))

    def mm_ssm(h):
        ins = None
        for k in range(16):
            ins = h.matmul(ps[:, 0, 0:256], ones[:], msq[:, k, :], start=(k == 0), stop=(k == 15))
        return ins
    sc.add("pe", mm_ssm, reads=("msq", "ones"), writes=pregs(0, 1))
    rsqrt_bc(ps[:, 0, 0:256], rstd_m, float(D), pregs(0, 1), "rstd_m", tmpm, "tmpm")
    for k in range(16):
        sc.add("dve", lambda h, k=k: h.scalar_tensor_tensor(out=memn[:, k, :], in0=memn[:, k, :], scalar=small[:, SM_MEMG + k:SM_MEMG + k + 1],
                                                          in1=rstd_m, op0=ALU.mult, op1=ALU.mult),
               reads=("memn", "rstd_m", "small"), writes=("memn",))
    for c in range(16):
        w, wr = wsm.next()
        w3 = w.rearrange("p (k c) -> p k c", k=16)
        bank = 1 + c % 3

        def mm_km(h, w3=w3, bank=bank):
            ins = None
            for k in range(16):
                ins = h.matmul(ps[:, bank, 0:256], w3[:, k, :], memn[:, k, :], start=(k == 0), stop=(k == 15))
            return ins
        sc.add("pe", mm_km, reads=(wr, "memn"), writes=pregs(bank, 1))
        sc.add("act", lambda h, c=c, bank=bank: h.activation(out=kmT[:, c, :], in_=ps[:, bank, 0:256], func=AF.Copy), reads=pregs(bank, 1), writes=("kmT",))
    for grp in range(4):
        for kq in range(4):
            w, wr = wsm.next()
            w3 = w.rearrange("p (k c) -> p k c", k=4)

            def mm_vm(h, w3=w3, kq=kq):
                ins = None
                for mb in range(2):
                    for kk in range(4):
                        ins = h.matmul(ps[:, 4 + mb, :], memn[:, kq * 4 + kk, mb * 128:(mb + 1) * 128], w3[:, kk, :],
                                       start=(kq == 0 and kk == 0), stop=(kq == 3 and kk == 3))
                return ins
            sc.add("pe", mm_vm, reads=(wr, "memn"), writes=pregs(4, 2))
        for mb in range(2):
            sc.add("dve", lambda h, mb=mb, grp=grp: h.tensor_copy(out=vm[:, mb, grp * 512:(grp + 1) * 512], in_=ps[:, 4 + mb, :]),
                   reads=pregs(4 + mb, 1), writes=("vm",))
    sc.barrier()
    for hm in range(4):
        for dc in range(4):
            w, wr = wsm.next()
            w3 = w.rearrange("p (k c) -> p k c", k=16)
            b = (dc % 2) * 2
            sc.add("pe", lambda h, b=b, w3=w3: mm32(b, w3, hT, h), reads=(wr,) + HT, writes=pregs(b))
            sc.add("act", lambda h, b=b, dc=dc: h.activation(out=qm[:, dc, :], in_=pair(b), func=AF.Copy, scale=MSC), reads=pregs(b), writes=(("qm", dc),))
        QM = tuple(("qm", dc) for dc in range(4))
        for mb in range(2):
            def mm_sm(h, hm=hm, mb=mb):
                ins = None
                for half in range(2):
                    for dc in range(4):
                        ins = h.matmul(ps[:, 4 + 2 * mb + half, :], kmT[:, hm * 4 + dc, mb * 128:(mb + 1) * 128], qm[:, dc, half * 512:(half + 1) * 512],
                                       start=(dc == 0), stop=(dc == 3))
                return ins
            sc.add("pe", mm_sm, reads=QM + ("kmT",), writes=pregs(4 + 2 * mb))
            sc.add("act", lambda h, mb=mb: h.activation(out=pm[:, mb, :], in_=pair(4 + 2 * mb), func=AF.Exp), reads=pregs(4 + 2 * mb), writes=(("pm", mb),))
        PM = (("pm", 0), ("pm", 1))

        def mm_rs(h):
            ins = None
            for half in range(2):
                for mb in range(2):
                    ins = h.matmul(ps[:, half, :], ones[:], pm[:, mb, half * 512:(half + 1) * 512], start=(mb == 0), stop=(mb == 1))
            return ins
        sc.add("pe", mm_rs, reads=PM + ("ones",), writes=pregs(0))
        sc.add("dve", lambda h: h.reciprocal(out=rs, in_=pair(0)), reads=pregs(0), writes=("rs",))
        for dc in range(4):
            c = hm * 4 + dc
            w, wr = wsm.next()
            w3 = w.rearrange("p (k c) -> p k c", k=16)
            sc.add("pe", lambda h, w3=w3: mm32(2, w3, hT, h), reads=(wr,) + HT, writes=pregs(2))
            sc.add("act", lambda h: h.activation(out=ztm, in_=pair(2), func=AF.Silu), reads=pregs(2), writes=("ztm",))
            yb_ = 4 + 2 * (dc % 2)

            def mm_ym(h, c=c, yb_=yb_):
                ins = None
                for half in range(2):
                    for mb in range(2):
                        ins = h.matmul(ps[:, yb_ + half, :], vm[:, mb, c * 128:(c + 1) * 128], pm[:, mb, half * 512:(half + 1) * 512],
                                       start=(mb == 0), stop=(mb == 1))
                return ins
            sc.add("pe", mm_ym, reads=PM + ("vm",), writes=pregs(yb_))
            sc.add("dve", lambda h, c=c, yb_=yb_: h.tensor_tensor(out=ym[:, c, :], in0=pair(yb_), in1=rs, op=ALU.mult),
                   reads=pregs(yb_) + ("rs",), writes=(("ym", c),))
            sc.add("dve", lambda h, c=c: h.tensor_tensor(out=ym[:, c, :], in0=ym[:, c, :], in1=ztm, op=ALU.mult),
                   reads=(("ym", c), "ztm"), writes=(("ym", c),))
    dump(6, ym[:, 0, :], [("ym", 0)])

    sc.barrier()

    def mg(c):
        return v2(A1, 32768 + c * 1024, 1024) if c < 8 else v2(SCR, (c - 8) * 1024, 1024)
    gt = v2(A5, 8192, 2048, F32)
    acc = v2(A5, 10240, 2048, F32)
    tmp4 = v2(A5, 12288, 2048, F32)
    plist = []
    for c in range(16):
        for n in range(3):
            plist += [PC["gate"] + n * 16 + c, PC["branch"] + n * 16 + c]
    ws4 = WStream("w4", [v2(A5, i * PIECE, PIECE) for i in range(4)], plist)
    ysrc = [(ya, "ya"), (yb, "yb"), (ym, "ym")]
    for c in range(16):
        for n in range(3):
            b = proj_chunk(ws4, hT, HT)
            sc.add("act", lambda h, b=b, n=n, c=c: h.activation(out=gt, in_=pair(b), func=AF.Sigmoid,
                                                              bias=small[:, SM_BGATE + n * 16 + c:SM_BGATE + n * 16 + c + 1], scale=1.0),
                   reads=pregs(b) + ("small",), writes=("gt",))
            ysb, yname = ysrc[n]
            b2 = proj_chunk(ws4, ysb, tuple((yname, k) for k in range(16)))
            if n == 0:
                sc.add("dve", lambda h, b2=b2: h.tensor_tensor(out=acc, in0=pair(b2), in1=gt, op=ALU.mult), reads=pregs(b2) + ("gt",), writes=("acc",))
            else:
                sc.add("dve", lambda h, b2=b2: h.tensor_tensor(out=tmp4, in0=pair(b2), in1=gt, op=ALU.mult), reads=pregs(b2) + ("gt",), writes=("tmp4",))
                if n == 1:
                    sc.add("dve", lambda h: h.tensor_tensor(out=acc, in0=acc, in1=tmp4, op=ALU.add), reads=("acc", "tmp4"), writes=("acc",))
                else:
                    sc.add("dve", lambda h, c=c: h.tensor_tensor(out=mg(c), in0=acc, in1=tmp4, op=ALU.add), reads=("acc", "tmp4"), writes=(("mg", c),))
    dump(7, mg(0), [("mg", 0)])

    sc.barrier()
    gpb = v2(A2, 0, 4096, F32)
    junk = v2(A2, 4096, 1024, F32)
    xob = [v2(A3, i * 4096, 4096, F32) for i in range(2)]
    yo = [v2(A3, 8192 + i * 4096, 4096, F32) for i in range(2)]
    sc.dma("sp", lambda h, sem: h.dma_start(out=gpb, in_=rows_d[0:1, ROW_GPOST:ROW_GPOST + 2048].to_broadcast([P, 2048])).then_inc(sem, 16),
           "gpb", 1, writes=("gpb",))
    for i in range(16):
        load_piece(PC["out"] + i, v2(A1, i * PIECE, PIECE), ("wo", i), "wo%d" % i)

    def wo(grp, kq):
        return v3(A1, (grp * 4 + kq) * PIECE, 4, 512)
    MG = tuple(("mg", c) for c in range(16))
    for j in range(8):
        pj = j % 2
        sc.dma("sp", lambda h, sem, j=j, pj=pj: h.dma_start(out=xob[pj], in_=xo_d[j * 128:(j + 1) * 128, :]).then_inc(sem, 16),
               "xob%d" % pj, 1, writes=(("xob", pj),))
        sc.add("dve", lambda h, pj=pj: h.memset(stat[:, pj * 4:pj * 4 + 4], 0.0), writes=tuple(("ss5", pj * 4 + g) for g in range(4)))
        for grp in range(4):
            bank = pj * 4 + grp

            def mm_o(h, j=j, grp=grp, bank=bank):
                ins = None
                for k in range(16):
                    ins = h.matmul(ps[:, bank, :], mg(k)[:, j * 128:(j + 1) * 128], wo(grp, k // 4)[:, k % 4, :], start=(k == 0), stop=(k == 15))
                return ins
            sc.add("pe", mm_o, reads=MG + tuple(("wo", grp * 4 + q) for q in range(4)), writes=pregs(bank, 1))
            sc.add("act", lambda h, bank=bank: h.activation(out=junk, in_=ps[:, bank, :], func=AF.Square, accum_out=stat[:, bank:bank + 1]),
                   reads=pregs(bank, 1) + (("ss5", bank),), writes=("junk", ("ss5", bank)))
        c8, c10, c12 = 8 + pj, 10 + pj, 12 + pj
        sc.add("dve", lambda h, pj=pj, c8=c8: h.reduce_sum(out=stat[:, c8:c8 + 1], in_=stat[:, pj * 4:pj * 4 + 4], axis=AX.X),
               reads=tuple(("ss5", pj * 4 + g) for g in range(4)), writes=(("st5a", pj),))
        sc.add("act", lambda h, c8=c8, c10=c10: h.activation(out=stat[:, c10:c10 + 1], in_=stat[:, c8:c8 + 1], func=AF.Sqrt, bias=EPS, scale=1.0 / D),
               reads=(("st5a", pj),), writes=(("st5b", pj),))
        sc.add("dve", lambda h, c10=c10, c12=c12: h.reciprocal(out=stat[:, c12:c12 + 1], in_=stat[:, c10:c10 + 1]), reads=(("st5b", pj),), writes=(("st5c", pj),))
        for grp in range(4):
            bank = pj * 4 + grp
            sc.add("dve", lambda h, grp=grp, bank=bank, pj=pj, c12=c12: h.scalar_tensor_tensor(
                out=yo[pj][:, grp * 512:(grp + 1) * 512], in0=ps[:, bank, :], scalar=stat[:, c12:c12 + 1], in1=gpb[:, grp * 512:(grp + 1) * 512],
                op0=ALU.mult, op1=ALU.mult), reads=pregs(bank, 1) + (("st5c", pj), "gpb"), writes=(("yo", pj),))
        sc.add("dve", lambda h, pj=pj: h.tensor_tensor(out=yo[pj], in0=yo[pj], in1=xob[pj], op=ALU.add), reads=(("yo", pj), ("xob", pj)), writes=(("yo", pj),))
        sc.dma("sp", lambda h, sem, j=j, pj=pj: h.dma_start(out=y_d[j * 128:(j + 1) * 128, :], in_=yo[pj]).then_inc(sem, 16),
               "yout%d" % pj, 1, reads=(("yo", pj),))

    sc.finalize()
    build.last_sc = sc
    sem_names = ["eng:" + e for e in Sched.ENGS] + ["dma:" + k for k in sc.dma_keys]
    sems = {}
    for i, s in enumerate(sem_names):
        sems[s] = es.enter_context(nc.semaphore("s%d" % i))
    final = [("dma:yout0", 64), ("dma:yout1", 64)]
    for seg in dbg_segs:
        final.append(("dma:dbg%d" % seg, 16))
    with nc.Block() as block:
        sc.emit(nc, block, sems, final)
    es.close()
    return nc


def kernel(**inputs):
    per_core = prep_inputs(inputs)
    nc = build()
    in_maps = [d for _, d in per_core]
    res = run_bass_kernel_spmd(nc, in_maps, core_ids=list(range(NCORE)))
    out = np.zeros((1, S, D), np.float32)
    for (idx, _), r in zip(per_core, res.results):
        out[0, idx] = r["y"]
    return out
```

```python
import numpy as np
import concourse.bass as bass
import concourse.mybir as mybir
from concourse.bass_utils import run_bass_kernel_spmd

F32 = mybir.dt.float32
BF16 = mybir.dt.bfloat16
I32 = mybir.dt.int32
AF = mybir.ActivationFunctionType
ALU = mybir.AluOpType
AX = mybir.AxisListType

P = 128
S = 8192
D = 2048
NCORE = 8
TOK = 1024
EPS = 1e-6
PIECE = 2048
TWO_PI = float(2.0 * np.pi)
C1 = 6.28125
C2 = float(2.0 * np.pi - 6.28125)

PC = {}
_n = 0
for _name, _cnt in (("ckv", 6), ("cq", 4), ("head", 16), ("zb", 16), ("u", 16), ("za", 16),
                    ("v", 16), ("memk", 16), ("memv", 16), ("qm", 16), ("zm", 16),
                    ("gate", 48), ("branch", 48), ("out", 16)):
    PC[_name] = _n
    _n += _cnt
NPIECE = _n

SM_GPRE, SM_QNG, SM_KVG, SM_MEMG, SM_BGATE, SM_CST = 0, 16, 20, 24, 40, 88
SM_N = 96
ROW_LNG, ROW_LNB, ROW_BS, ROW_GPOST = 0, 2048, 4096, 6144


def _fm_pieces(W):
    K, C = W.shape
    kc = K // P
    n = C // P
    t = W.reshape(kc, P, n, P).transpose(2, 1, 0, 3)
    if kc == 16:
        return np.ascontiguousarray(t).reshape(n, P, PIECE)
    raise ValueError


def _tm_pieces(W):
    K, C = W.shape
    g = C // 512
    t = W.reshape(4, 4, P, g, 512).transpose(3, 0, 2, 1, 4)
    return np.ascontiguousarray(t).reshape(g * 4, P, PIECE)


def prep_inputs(inp):
    f = np.float32
    x = np.asarray(inp["x"], f)[0]
    mem = np.asarray(inp["mem"], f)[0]
    pos = np.asarray(inp["positions"], np.int32)
    w_in = np.asarray(inp["w_in"], f)[0]
    o = np.cumsum([0, 2048, 2048, 2048, 512, 512, 64, 2048, 2048, 2048])
    wu, wv, wza, wcq, wckv, wkr, wzb, wqm, wzm = [w_in[:, o[i]:o[i + 1]] for i in range(9)]
    sw = np.concatenate([np.arange(32, 64), np.arange(0, 32)])
    pieces = np.empty((NPIECE, P, PIECE), f)
    ckv_cols = np.concatenate([wckv, wkr, wkr, wkr[:, sw], wkr[:, sw]], axis=1)
    pieces[PC["ckv"]:PC["ckv"] + 6] = _fm_pieces(ckv_cols)
    pieces[PC["cq"]:PC["cq"] + 4] = _fm_pieces(wcq)
    w_uq = np.asarray(inp["w_uq"], f)[0].reshape(512, 16, 192)
    w_ukv = np.asarray(inp["w_ukv"], f)[0].reshape(512, 16, 256)
    for h in range(16):
        rope = w_uq[:, h, 128:192]
        hw = np.concatenate([w_uq[:, h, 0:128], rope, rope[:, sw], w_ukv[:, h, 0:128], w_ukv[:, h, 128:256]], axis=1)
        pieces[PC["head"] + h] = hw.reshape(4, P, 512).transpose(1, 0, 2).reshape(P, PIECE)
    pieces[PC["zb"]:PC["zb"] + 16] = _fm_pieces(wzb)
    pieces[PC["u"]:PC["u"] + 16] = _fm_pieces(wu)
    pieces[PC["za"]:PC["za"] + 16] = _fm_pieces(wza)
    pieces[PC["v"]:PC["v"] + 16] = _tm_pieces(wv)
    wmkv = np.asarray(inp["w_mem_kv"], f)[0]
    pieces[PC["memk"]:PC["memk"] + 16] = _fm_pieces(wmkv[:, 0:2048])
    pieces[PC["memv"]:PC["memv"] + 16] = _tm_pieces(wmkv[:, 2048:4096])
    pieces[PC["qm"]:PC["qm"] + 16] = _fm_pieces(wqm)
    pieces[PC["zm"]:PC["zm"] + 16] = _fm_pieces(wzm)
    wg = np.asarray(inp["w_gate"], f)[0]
    wb = np.asarray(inp["w_branch"], f)[0]
    for n in range(3):
        pieces[PC["gate"] + 16 * n:PC["gate"] + 16 * n + 16] = _fm_pieces(wg[:, n * 2048:(n + 1) * 2048])
        pieces[PC["branch"] + 16 * n:PC["branch"] + 16 * n + 16] = _fm_pieces(wb[n])
    pieces[PC["out"]:PC["out"] + 16] = _tm_pieces(np.asarray(inp["w_out"], f)[0])

    small = np.zeros((P, SM_N), f)
    small[:, SM_GPRE:SM_GPRE + 16] = np.asarray(inp["g_pre"], f)[0].reshape(16, P).T
    small[:, SM_QNG:SM_QNG + 4] = np.asarray(inp["q_norm_g"], f)[0].reshape(4, P).T
    small[:, SM_KVG:SM_KVG + 4] = np.asarray(inp["kv_norm_g"], f)[0].reshape(4, P).T
    small[:, SM_MEMG:SM_MEMG + 16] = np.asarray(inp["mem_norm_g"], f)[0].reshape(16, P).T
    small[:, SM_BGATE:SM_BGATE + 48] = np.asarray(inp["b_gate"], f)[0].reshape(48, P).T
    inv_freq = (1.0 / (np.float32(10000.0) ** (np.arange(0, 64, 2, dtype=f) / np.float32(64)))).astype(f)
    pidx = np.arange(P)
    small[:, SM_CST + 0] = inv_freq[pidx % 32]
    small[:, SM_CST + 1] = np.where((pidx % 64) < 32, -1.0, 1.0)
    rows = np.concatenate([np.asarray(inp["a_ln_g"], f)[0], np.asarray(inp["a_ln_b"], f)[0],
                           np.asarray(inp["a_b_s"], f)[0].reshape(-1), np.asarray(inp["g_post"], f)[0]])[None]
    wsT = np.ascontiguousarray(np.asarray(inp["a_w_s"], f)[0].transpose(2, 0, 1)).reshape(P, 16 * P)
    ident = np.eye(P, dtype=f)
    xaT = np.ascontiguousarray(x.T.reshape(16, P, 16, 512).transpose(2, 1, 0, 3)).reshape(16, P, 8192)
    memT = np.ascontiguousarray(mem.T.reshape(16, P, 256).transpose(1, 0, 2)).reshape(P, 4096)
    shared = {"xaT": xaT, "pos_all": np.ascontiguousarray(pos.reshape(1, S)), "memT": memT, "wts": pieces,
              "small": small, "rows": np.ascontiguousarray(rows), "wsT": wsT, "ident": ident,
              "tri": (np.arange(P)[:, None] <= np.arange(P)[None, :]).astype(f)}
    per_core = []
    kk = np.arange(P)[:, None]
    qq = np.arange(P)[None, :]
    for c in range(NCORE):
        idx = np.concatenate([np.arange((c + 8 * j) * P, (c + 8 * j + 1) * P) for j in range(8)])
        xo = np.ascontiguousarray(x[idx])
        xoT = np.ascontiguousarray(xo.T.reshape(16, P, TOK).transpose(1, 0, 2)).reshape(P, 16 * TOK)
        mask = np.zeros((P, 8, P), f)
        for oo in range(8):
            if oo < c:
                mask[:, oo, :] = 1.0
            elif oo == c:
                mask[:, oo, :] = (kk <= qq).astype(f)
        d = dict(shared)
        d.update({"xo": xo, "xoT": xoT, "pos_own": np.ascontiguousarray(pos[0, idx].reshape(1, TOK)),
                  "mask": ((mask - 1.0) * 30000.0).astype(f).reshape(P, 8 * P)})
        per_core.append((idx, d))
    return per_core


class _Op:
    __slots__ = ("eng", "fn", "reads", "writes", "dma", "key", "ndma", "deps", "needed", "token", "waits", "idx")


class Sched:
    ENGS = ("pe", "act", "dve", "pool", "sp")

    def __init__(self):
        self.ops = []
        self.seen = set()
        self.bar_fn = None

    def barrier(self):
        regs = tuple(self.seen)
        self.add("dve", self.bar_fn, reads=regs, writes=regs + ("BAR",))

    def add(self, eng, fn, reads=(), writes=()):
        op = _Op()
        op.eng, op.fn, op.reads, op.writes = eng, fn, tuple(reads) + ("BAR",), tuple(writes)
        op.dma, op.key, op.ndma = False, None, 0
        self.seen.update(op.reads)
        self.seen.update(op.writes)
        op.idx = len(self.ops)
        self.ops.append(op)
        return op

    def dma(self, eng, fn, key, n, reads=(), writes=()):
        op = self.add(eng, fn, reads, writes)
        op.dma, op.key, op.ndma = True, key, n
        return op

    def finalize(self):
        last_w, readers = {}, {}
        for op in self.ops:
            deps = {}
            for r in op.reads:
                if r in last_w:
                    deps[last_w[r]] = "raw"
            for w in op.writes:
                if w in last_w:
                    deps.setdefault(last_w[w], "waw")
                for rd in readers.get(w, ()):
                    deps.setdefault(rd, "war")
            deps.pop(op.idx, None)
            op.deps = deps
            for r in op.reads:
                readers.setdefault(r, []).append(op.idx)
            for w in op.writes:
                last_w[w] = op.idx
                readers[w] = []
        for op in self.ops:
            keep = {}
            for i, kind in op.deps.items():
                src = self.ops[i]
                if (not src.dma) and (not op.dma) and src.eng == op.eng:
                    if op.eng == "pe":
                        continue
                keep[i] = kind
            op.deps = keep
        for op in self.ops:
            op.needed = False
        for op in self.ops:
            for i in op.deps:
                self.ops[i].needed = True
        cnt = {e: 0 for e in self.ENGS}
        dcnt = {}
        for op in self.ops:
            if op.dma:
                dcnt[op.key] = dcnt.get(op.key, 0) + 16 * op.ndma
                op.token = ("dma:" + op.key, dcnt[op.key])
            else:
                if op.needed:
                    cnt[op.eng] += 1
                op.token = ("eng:" + op.eng, cnt[op.eng])
        self.dma_keys = sorted(dcnt)
        waited = {e: {} for e in self.ENGS}
        for op in self.ops:
            need = {}
            for i in op.deps:
                s, v = self.ops[i].token
                if v > need.get(s, 0):
                    need[s] = v
            op.waits = []
            for s, v in need.items():
                if waited[op.eng].get(s, 0) < v:
                    waited[op.eng][s] = v
                    op.waits.append((s, v))

    def emit(self, nc, block, sems, final_waits):
        by_eng = {e: [op for op in self.ops if op.eng == e] for e in self.ENGS}

        def run(e, h):
            for op in by_eng[e]:
                for s, v in op.waits:
                    h.wait_ge(sems[s], v)
                if op.dma:
                    op.fn(h, sems["dma:" + op.key])
                else:
                    ins = op.fn(h)
                    if op.needed:
                        ins.then_inc(sems["eng:" + e], 1)
            if e == "sp":
                for s, v in final_waits:
                    h.wait_ge(sems[s], v)

        @block.tensor
        def _(h):
            run("pe", h)

        @block.scalar
        def _(h):
            run("act", h)

        @block.vector
        def _(h):
            run("dve", h)

        @block.gpsimd
        def _(h):
            run("pool", h)

        @block.sync
        def _(h):
            run("sp", h)


def build(stop_after=None, dbg=None):
    from contextlib import ExitStack
    nc = bass.Bass("TRN2", target_bir_lowering=False)
    dt = nc.dram_tensor
    xaT = dt("xaT", [16, P, 8192], F32, kind="ExternalInput").ap()
    pos_all = dt("pos_all", [1, S], I32, kind="ExternalInput").ap()
    memT = dt("memT", [P, 4096], F32, kind="ExternalInput").ap()
    wts = dt("wts", [NPIECE, P, PIECE], F32, kind="ExternalInput").ap()
    small_d = dt("small", [P, SM_N], F32, kind="ExternalInput").ap()
    rows_d = dt("rows", [1, 8192], F32, kind="ExternalInput").ap()
    wsT_d = dt("wsT", [P, 2048], F32, kind="ExternalInput").ap()
    ident_d = dt("ident", [P, P], F32, kind="ExternalInput").ap()
    xo_d = dt("xo", [TOK, D], F32, kind="ExternalInput").ap()
    xoT_d = dt("xoT", [P, 16 * TOK], F32, kind="ExternalInput").ap()
    pos_own = dt("pos_own", [1, TOK], I32, kind="ExternalInput").ap()
    mask_d = dt("mask", [P, 1024], F32, kind="ExternalInput").ap()
    tri_d = dt("tri", [P, P], F32, kind="ExternalInput").ap()
    y_d = dt("y", [TOK, D], F32, kind="ExternalOutput").ap()
    dbg_d = dt("dbg", [P, 8192], F32, kind="ExternalOutput").ap() if dbg else None

    es = ExitStack()
    sb = lambda name, shape, dtype: es.enter_context(nc.sbuf_tensor(name, shape, dtype))
    A1 = sb("A1", [P, 40960], BF16)
    A2 = sb("A2", [P, 16896], BF16)
    A3 = sb("A3", [P, 16384], BF16)
    A5 = sb("A5", [P, 14336], BF16)
    SCR = sb("SCR", [P, 8192], BF16)
    small = sb("smallc", [P, SM_N], F32)
    ones = sb("ones", [P, P], BF16)
    ident = sb("identb", [P, P], BF16)
    rstd_o = sb("rstd_o", [P, TOK], F32)
    Tq = sb("Tq", [P, TOK], F32)
    stat = sb("stat", [P, 128], F32)
    tri = sb("trib", [P, P], BF16)
    bart = sb("bart", [P, 8], F32)
    ps = es.enter_context(nc.psum_tensor("ps", [P, 8, 512], F32))

    def v2(arena, off, n, dtype=BF16):
        a = arena[:, off:off + n]
        return a if dtype == BF16 else a.bitcast(dtype)

    def v3(arena, off, a, b, dtype=BF16):
        n = a * b * (2 if dtype != BF16 else 1)
        return v2(arena, off, n, dtype).rearrange("p (a b) -> p a b", a=a)

    def pair(b):
        return ps[:, b:b + 2, :].rearrange("p b c -> p (b c)")

    sc = Sched()
    sc.bar_fn = lambda h: h.memset(bart[:], 0.0)
    cst = small[:, SM_CST:SM_CST + 8]
    invf = cst[:, 0:1]
    sgn = cst[:, 1:2]

    def load_piece(idx, dst_ap, region, key):
        def fn(h, sem, idx=idx, dst_ap=dst_ap):
            h.dma_start(out=dst_ap, in_=wts[idx]).then_inc(sem, 16)
        sc.dma("pool", fn, key, 1, writes=(region,))

    class WStream:
        def __init__(self, name, bufs, plist):
            self.name, self.bufs, self.plist = name, bufs, plist
            self.issued = 0
            self.used = 0

        def _issue_upto(self, n):
            while self.issued < min(n, len(self.plist)):
                i = self.issued
                b = i % len(self.bufs)
                load_piece(self.plist[i], self.bufs[b], (self.name, b), "%s%d" % (self.name, b))
                self.issued += 1

        def next(self):
            i = self.used
            self._issue_upto(i + len(self.bufs))
            self.used += 1
            b = i % len(self.bufs)
            return self.bufs[b], (self.name, b)

        def prefetch(self):
            self._issue_upto(self.used + len(self.bufs))

    def rsqrt_bc(src_ps_ap, dst_ap, n_feat, src_regions, dst_region, tmp_ap, tmp_region):
        sc.add("act", lambda h: h.activation(out=tmp_ap, in_=src_ps_ap, func=AF.Sqrt, bias=EPS, scale=1.0 / n_feat),
               reads=src_regions, writes=(tmp_region,))
        sc.add("dve", lambda h: h.reciprocal(out=dst_ap, in_=tmp_ap), reads=(tmp_region,), writes=(dst_region,))

    def rope_tables(posi, pos_src_ap, n, scratch, names, outs):
        a, kf, m = scratch
        ra, rk, rm, rp = names
        sc.dma("sp", lambda h, sem: h.dma_start(out=posi, in_=pos_src_ap.to_broadcast([P, n])).then_inc(sem, 16),
               rp, 1, writes=(rp,))
        sc.add("dve", lambda h: h.tensor_scalar(out=a, in0=posi, scalar1=invf, scalar2=None, op0=ALU.mult), reads=(rp, "small"), writes=(ra,))
        sc.add("dve", lambda h: h.tensor_scalar(out=posi, in0=a, scalar1=1.0 / TWO_PI, scalar2=None, op0=ALU.mult), reads=(ra,), writes=(rp,))
        sc.add("dve", lambda h: h.tensor_copy(out=kf, in_=posi), reads=(rp,), writes=(rk,))
        sc.add("dve", lambda h: h.scalar_tensor_tensor(out=a, in0=kf, scalar=-C1, in1=a, op0=ALU.mult, op1=ALU.add), reads=(rk, ra), writes=(ra,))
        sc.add("dve", lambda h: h.scalar_tensor_tensor(out=a, in0=kf, scalar=-C2, in1=a, op0=ALU.mult, op1=ALU.add), reads=(rk, ra), writes=(ra,))
        sc.add("dve", lambda h: h.tensor_scalar(out=m, in0=a, scalar1=float(np.pi), scalar2=-TWO_PI, op0=ALU.is_gt, op1=ALU.mult), reads=(ra,), writes=(rm,))
        sc.add("dve", lambda h: h.tensor_tensor(out=a, in0=a, in1=m, op=ALU.add), reads=(ra, rm), writes=(ra,))
        sc.add("dve", lambda h: h.tensor_scalar(out=m, in0=a, scalar1=float(-np.pi), scalar2=TWO_PI, op0=ALU.is_lt, op1=ALU.mult), reads=(ra,), writes=(rm,))
        sc.add("dve", lambda h: h.tensor_tensor(out=a, in0=a, in1=m, op=ALU.add), reads=(ra, rm), writes=(ra,))
        sc.add("dve", lambda h: h.scalar_tensor_tensor(out=kf, in0=a, scalar=-1.0, in1=a, op0=ALU.mult, op1=ALU.min), reads=(ra,), writes=(rk,))
        for kind, out_ap, psl, region in outs:
            if kind == "cos":
                sc.add("act", lambda h, out_ap=out_ap, psl=psl: h.activation(out=out_ap, in_=kf[psl], func=AF.Sin, bias=halfpi[psl], scale=1.0),
                       reads=(rk, "halfpi"), writes=(region,))
            else:
                sc.add("act", lambda h, out_ap=out_ap, psl=psl: h.activation(out=out_ap, in_=a[psl], func=AF.Sin, scale=sgn[psl]),
                       reads=(ra, "small"), writes=(region,))

    halfpi = stat[:, 127:128]

    def setup_fn(h, sem):
        h.dma_start(out=small[:], in_=small_d[:, :]).then_inc(sem, 16)
    sc.dma("sp", setup_fn, "small", 1, writes=("small",))
    sc.dma("pool", lambda h, sem: h.dma_start(out=ident[:], in_=ident_d[:, :]).then_inc(sem, 16), "ident", 1, writes=("ident",))
    sc.add("dve", lambda h: h.memset(ones[:], 1.0), writes=("ones",))
    sc.add("dve", lambda h: h.memset(halfpi, float(np.pi / 2)), writes=("halfpi",))
    SMALL = ("small", "halfpi")

    dbg_state = {"done": False}

    def finish_dbg(view_ap, region, nfree, is_f32):
        def fn(h, sem):
            h.dma_start(out=dbg_d[:, 0:nfree], in_=view_ap).then_inc(sem, 16)
        sc.dma("pool", fn, "dbgout", 1, reads=(region,) if not isinstance(region, list) else tuple(region))
        dbg_state["done"] = True

    ckvn = v3(A1, 0, 4, 8192)
    kpe2 = v2(A1, 32768, 8192)
    xa = [v3(A3, 0, 16, 512), v3(A3, 8192, 16, 512)]
    sq0 = v3(A2, 0, 16, 512)
    ckvf = v3(A2, 8192, 6, 512, F32)
    sq2 = v3(A2, 14336, 4, 512)
    wckv = [v3(A5, i * PIECE, 16, 128) for i in range(6)]
    f32s = lambda i: v2(SCR, i * 1024, 1024, F32)
    cosK, ssK, t_a, t_k, t_m, rstd_bc, rt_tmp = [f32s(i) for i in range(7)]
    posi0 = v2(SCR, 7 * 1024, 1024, I32)

    for i in range(6):
        load_piece(PC["ckv"] + i, v2(A5, i * PIECE, PIECE), ("wckv", i), "wckv%d" % i)
    for i in range(6):
        for k in range(16):
            sc.add("dve", lambda h, i=i, k=k: h.tensor_scalar(out=wckv[i][:, k, :], in0=wckv[i][:, k, :],
                                                            scalar1=small[:, SM_GPRE + k:SM_GPRE + k + 1], scalar2=None, op0=ALU.mult),
                   reads=(("wckv", i), "small"), writes=(("wckv", i),))

    def load_xa(n):
        b = n % 2
        def fn(h, sem, n=n, b=b):
            h.dma_start(out=v2(A3, b * 8192, 8192), in_=xaT[n]).then_inc(sem, 16)
        sc.dma("pool", fn, "xa%d" % b, 1, writes=(("xa", b),))

    load_xa(0)
    NT0 = 16

    def do_rope0(n):
        tsl_ = slice(n * 512, (n + 1) * 512)
        rope_tables(posi0, pos_all[0:1, tsl_], 512, (t_a, t_k, t_m), ("t_a", "t_k", "t_m", "posi0"),
                    [("cos", cosK, slice(0, P), "cosK"), ("ss", ssK, slice(0, P), "ssK")])

    def do_sq0(n):
        b_ = n % 2
        sc.add("act", lambda h, b_=b_: h.activation(out=sq0, in_=xa[b_], func=AF.Square), reads=(("xa", b_),), writes=("sq0",))

    do_rope0(0)
    do_sq0(0)
    for n in range(NT0):
        b = n % 2
        if n + 1 < NT0:
            load_xa(n + 1)
        tsl = slice(n * 512, (n + 1) * 512)

        def mm_ss(h, b=b):
            ins = None
            for k in range(16):
                ins = h.matmul(ps[:, 6, :], ones[:], sq0[:, k, :], start=(k == 0), stop=(k == 15))
            return ins
        sc.add("pe", mm_ss, reads=("sq0", "ones"), writes=("ps6",))
        rsqrt_bc(ps[:, 6, :], rstd_bc, float(D), ("ps6",), "rstd_bc", rt_tmp, "rt_tmp")
        for c in range(6):
            bank = c

            def mm_c(h, c=c, b=b, bank=bank):
                ins = None
                for k in range(16):
                    ins = h.matmul(ps[:, bank, :], wckv[c][:, k, :], xa[b][:, k, :], start=(k == 0), stop=(k == 15))
                return ins
            sc.add("pe", mm_c, reads=(("wckv", c), ("xa", b)), writes=("ps%d" % bank,))
            sc.add("dve", lambda h, c=c, bank=bank: h.tensor_tensor(out=ckvf[:, c, :], in0=ps[:, bank, :], in1=rstd_bc, op=ALU.mult),
                   reads=("ps%d" % bank, "rstd_bc"), writes=(("ckvf", c),))
        if n + 1 < NT0:
            do_sq0(n + 1)
        sc.add("act", lambda h: h.activation(out=sq2, in_=ckvf[:, 0:4, :], func=AF.Square),
               reads=tuple(("ckvf", c) for c in range(4)), writes=("sq2",))

        def mm_ss2(h):
            ins = None
            for k in range(4):
                ins = h.matmul(ps[:, 7, :], ones[:], sq2[:, k, :], start=(k == 0), stop=(k == 3))
            return ins
        sc.add("pe", mm_ss2, reads=("sq2", "ones"), writes=("ps7",))
        rsqrt_bc(ps[:, 7, :], rt_tmp, 512.0, ("ps7",), "rt_tmp", t_m, "t_m")
        for k in range(4):
            sc.add("dve", lambda h, k=k, tsl=tsl: h.scalar_tensor_tensor(out=ckvn[:, k, tsl], in0=ckvf[:, k, :],
                                                                      scalar=small[:, SM_KVG + k:SM_KVG + k + 1], in1=rt_tmp,
                                                                      op0=ALU.mult, op1=ALU.mult),
                   reads=(("ckvf", k), "rt_tmp", "small"), writes=(("ckvn", n),))
        sc.add("dve", lambda h: h.tensor_tensor(out=t_a, in0=ckvf[:, 4, :], in1=cosK, op=ALU.mult), reads=(("ckvf", 4), "cosK"), writes=("t_a",))
        sc.add("dve", lambda h: h.tensor_tensor(out=t_k, in0=ckvf[:, 5, :], in1=ssK, op=ALU.mult), reads=(("ckvf", 5), "ssK"), writes=("t_k",))
        sc.add("dve", lambda h, tsl=tsl: h.tensor_tensor(out=kpe2[:, tsl], in0=t_a, in1=t_k, op=ALU.add), reads=("t_a", "t_k"), writes=(("kpe2", n),))
        if n + 1 < NT0:
            do_rope0(n + 1)

    def dump(seg, view_ap, regions):
        if not dbg:
            return
        def fn(h, sem, seg=seg, view_ap=view_ap):
            h.dma_start(out=dbg_d[:, seg * 1024:(seg + 1) * 1024], in_=view_ap).then_inc(sem, 16)
        sc.dma("pool", fn, "dbg%d" % seg, 1, reads=tuple(regions))
        dbg_segs.append(seg)

    dbg_segs = []
    dump(0, ckvn[:, 0, 0:1024], [("ckvn", 0), ("ckvn", 1)])

    def mm32(out_pair_b, w3, act3, h):
        ins = None
        for k in range(16):
            for half in range(2):
                ins = h.matmul(ps[:, out_pair_b + half, :], w3[:, k, :], act3[:, k, half * 512:(half + 1) * 512],
                               start=(k == 0), stop=(k == 15))
        return ins

    def pregs(b, n=2):
        return tuple("ps%d" % (b + i) for i in range(n))

    sc.barrier()
    hT = v3(A2, 0, 16, 1024)
    HT = tuple(("hT", k) for k in range(16))
    sqo = v3(A3, 0, 16, 1024)
    cqf = v3(A3, 0, 4, 1024, F32)
    sq2o = v3(A3, 8192, 4, 1024)
    cqn = v3(A5, 0, 4, 1024)
    o_slots = [v2(SCR, i * 2048, 2048, F32) for i in range(4)]
    posi1 = v2(SCR, 0, 2048, I32)
    o_a, o_k, o_m = o_slots[1], o_slots[2], o_slots[3]

    def load_hT():
        def fn(h, sem):
            for q in range(4):
                h.dma_start(out=v2(A2, q * 4096, 4096), in_=xoT_d[:, q * 4096:(q + 1) * 4096]).then_inc(sem, 16)
        sc.dma("pool", fn, "hTld", 4, writes=HT)

    def scale_hT():
        for k in range(16):
            sc.add("dve", lambda h, k=k: h.scalar_tensor_tensor(out=hT[:, k, :], in0=hT[:, k, :], scalar=small[:, SM_GPRE + k:SM_GPRE + k + 1],
                                                              in1=rstd_o[:], op0=ALU.mult, op1=ALU.mult),
                   reads=(("hT", k), "rstd_o", "small"), writes=(("hT", k),))

    load_hT()
    sc.add("dve", lambda h: h.tensor_tensor(out=sqo, in0=hT, in1=hT, op=ALU.mult), reads=HT, writes=("sqo",))

    def mm_sso(h):
        ins = None
        for k in range(16):
            for half in range(2):
                ins = h.matmul(ps[:, half, :], ones[:], sqo[:, k, half * 512:(half + 1) * 512], start=(k == 0), stop=(k == 15))
        return ins
    sc.add("pe", mm_sso, reads=("sqo", "ones"), writes=pregs(0))
    rsqrt_bc(pair(0), rstd_o[:], float(D), pregs(0), "rstd_o", o_m, "o_m")
    scale_hT()
    rope_tables(posi1, pos_own[0:1, :], 1024, (o_a, o_k, o_m), ("o_a", "o_k", "o_m", "posi1"),
                [("cos", Tq[0:64, :], slice(0, 64), "Tq0"), ("ss", Tq[64:128, :], slice(64, 128), "Tq1")])
    ws1 = WStream("w1", [v2(A5, 4096 + i * PIECE, PIECE) for i in range(5)], [PC["cq"] + i for i in range(4)])
    for c in range(4):
        w, wr = ws1.next()
        w3 = w.rearrange("p (k c) -> p k c", k=16)
        b = (c % 2) * 2
        sc.add("pe", lambda h, b=b, w3=w3: mm32(b, w3, hT, h), reads=(wr,) + HT, writes=pregs(b))
        sc.add("act", lambda h, b=b, c=c: h.activation(out=cqf[:, c, :], in_=pair(b), func=AF.Copy), reads=pregs(b), writes=(("cqf", c),))
    CQF = tuple(("cqf", c) for c in range(4))
    sc.add("act", lambda h: h.activation(out=sq2o, in_=cqf, func=AF.Square), reads=CQF, writes=("sq2o",))

    def mm_ssq(h):
        ins = None
        for k in range(4):
            for half in range(2):
                ins = h.matmul(ps[:, 4 + half, :], ones[:], sq2o[:, k, half * 512:(half + 1) * 512], start=(k == 0), stop=(k == 3))
        return ins
    sc.add("pe", mm_ssq, reads=("sq2o", "ones"), writes=pregs(4))
    rsqrt_bc(pair(4), o_a, 512.0, pregs(4), "o_a", o_k, "o_k")
    for k in range(4):
        sc.add("dve", lambda h, k=k: h.scalar_tensor_tensor(out=cqn[:, k, :], in0=cqf[:, k, :], scalar=small[:, SM_QNG + k:SM_QNG + k + 1],
                                                          in1=o_a, op0=ALU.mult, op1=ALU.mult),
               reads=(("cqf", k), "o_a", "small"), writes=(("cqn", k),))
    CQN = tuple(("cqn", k) for k in range(4))
    dump(1, hT[:, 0, :], [("hT", 0)])
    dump(2, cqn[:, 0, :], [("cqn", 0)])

    sc.barrier()
    qn = v2(A5, 4096, 1024)
    qr = v2(A5, 5120, 1024)
    pT = [v2(A5, 6144 + i * 1024, 1024) for i in range(3)]
    maskb = v3(A5, 9216, 8, 128)
    kT = v2(A2, 0, 8192)
    vtok = v3(A2, 8192, 64, 128)
    yb = v3(A3, 0, 16, 1024)
    accD = v2(SCR, 0, 2048, F32)
    accP = v2(SCR, 2048, 2048, F32)
    hi_b = v2(SCR, 4096, 1024)
    lo_b = v2(SCR, 5120, 1024)
    rsb = v2(SCR, 6144, 2048, F32)
    QSC = float(192.0 ** -0.5)
    sc.dma("pool", lambda h, sem: h.dma_start(out=v2(A5, 9216, 1024), in_=mask_d[:, :]).then_inc(sem, 16), "mask", 1, writes=("maskb",))
    sc.dma("pool", lambda h, sem: h.dma_start(out=tri[:], in_=tri_d[:, :]).then_inc(sem, 16), "tri", 1, writes=("tri",))
    wsh = WStream("wh", [v2(A5, 10240 + i * PIECE, PIECE) for i in range(2)], [PC["head"] + h for h in range(16)])

    def oacc(j):
        return ps[:, 4 + j // 3, (j % 3) * 129:(j % 3) * 129 + 129]

    def finalize_head(hd):
        sc.add("dve", lambda h: h.tensor_copy(out=hi_b, in_=accD), reads=("accD",), writes=("hi_b",))
        sc.add("dve", lambda h: h.tensor_tensor(out=accD, in0=accD, in1=hi_b, op=ALU.subtract), reads=("accD", "hi_b"), writes=("accD",))
        sc.add("dve", lambda h: h.tensor_copy(out=lo_b, in_=accD), reads=("accD",), writes=("lo_b",))

        def mm_sum(h):
            ins = None
            for half in range(2):
                h.matmul(ps[:, half, :], ones[:], hi_b[:, half * 512:(half + 1) * 512], start=True, stop=False)
                ins = h.matmul(ps[:, half, :], ones[:], lo_b[:, half * 512:(half + 1) * 512], start=False, stop=True)
            return ins
        sc.add("pe", mm_sum, reads=("hi_b", "lo_b", "ones"), writes=pregs(0, 2))
        sc.add("dve", lambda h: h.reciprocal(out=rsb, in_=pair(0)), reads=pregs(0, 2), writes=("rsb",))
        sc.add("dve", lambda h, hd=hd: h.tensor_tensor(out=yb[:, hd, :], in0=pair(6), in1=rsb, op=ALU.mult), reads=pregs(6, 2) + ("rsb",), writes=(("yb", hd),))

    NHEAD = 16
    for hd in range(NHEAD):
        w, wr = wsh.next()
        wh = w.rearrange("p (k c) -> p k c", k=4)

        def mm_q(h, wh=wh, lo=0, b=0):
            ins = None
            for k in range(4):
                for half in range(2):
                    ins = h.matmul(ps[:, b + half, :], wh[:, k, lo:lo + 128], cqn[:, k, half * 512:(half + 1) * 512], start=(k == 0), stop=(k == 3))
            return ins
        sc.add("pe", lambda h, wh=wh: mm_q(h, wh, 0, 0), reads=(wr,) + CQN, writes=pregs(0))
        sc.add("act", lambda h: h.activation(out=qn, in_=pair(0), func=AF.Copy, scale=QSC), reads=pregs(0), writes=("qn",))
        sc.add("pe", lambda h, wh=wh: mm_q(h, wh, 128, 2), reads=(wr,) + CQN, writes=pregs(2))
        sc.add("dve", lambda h: h.scalar_tensor_tensor(out=qr, in0=pair(2), scalar=QSC, in1=Tq[:], op0=ALU.mult, op1=ALU.mult),
               reads=pregs(2) + ("Tq0", "Tq1"), writes=("qr",))
        for t in range(16):
            bank = t % 4

            def mm_k(h, wh=wh, t=t, bank=bank):
                ins = None
                for k in range(4):
                    ins = h.matmul(ps[:, bank, :], wh[:, k, 256:384], ckvn[:, k, t * 512:(t + 1) * 512], start=(k == 0), stop=(k == 3))
                return ins
            sc.add("pe", mm_k, reads=(wr, ("ckvn", t)), writes=pregs(bank, 1))
            sc.add("act", lambda h, t=t, bank=bank: h.activation(out=kT[:, t * 512:(t + 1) * 512], in_=ps[:, bank, :], func=AF.Copy),
                   reads=pregs(bank, 1), writes=(("kT", t),))
        for g4 in range(16):
            bank = g4 % 4

            def mm_v(h, wh=wh, g4=g4, bank=bank):
                ins = None
                for bi in range(4):
                    blk = g4 * 4 + bi
                    for k in range(4):
                        ins = h.matmul(ps[:, bank, bi * 128:(bi + 1) * 128], ckvn[:, k, blk * 128:(blk + 1) * 128], wh[:, k, 384:512],
                                       start=(k == 0), stop=(k == 3))
                return ins
            sc.add("pe", mm_v, reads=(wr, ("ckvn", g4)), writes=pregs(bank, 1))
            sc.add("act" if g4 % 2 == 0 else "dve", (lambda h, g4=g4, bank=bank: h.activation(out=vtok[:, g4 * 4:(g4 + 1) * 4, :],
                                                                   in_=ps[:, bank, :].rearrange("p (a b) -> p a b", a=4), func=AF.Copy)) if g4 % 2 == 0 else
                   (lambda h, g4=g4, bank=bank: h.tensor_copy(out=vtok[:, g4 * 4:(g4 + 1) * 4, :], in_=ps[:, bank, :].rearrange("p (a b) -> p a b", a=4))),
                   reads=pregs(bank, 1), writes=(("vaug", g4),))

        def qk(kb):
            G = kb // 8
            pb = (kb % 3) * 2
            if G < 4:
                rngs = [(pb, G * 128, 512, 0), (pb + 1, 512, 1024, 512)]
            else:
                rngs = [(pb + 1, G * 128, 1024, 512)]

            def fn(h, kb=kb, rngs=rngs):
                ins = None
                for ri, (bank, c0, c1, off) in enumerate(rngs):
                    h.matmul(ps[:, bank, c0 - off:c1 - off], kT[:, kb * 128:(kb + 1) * 128], qn[:, c0:c1], start=True, stop=False)
                    if ri == 0:
                        h.matmul(ps[:, bank, c0 - off:c0 - off + 128], ident[:], maskb[:, kb % 8, :], start=False, stop=False)
                    ins = h.matmul(ps[:, bank, c0 - off:c1 - off], kpe2[:, kb * 128:(kb + 1) * 128], qr[:, c0:c1], start=False, stop=True)
                return ins
            sc.add("pe", fn, reads=(("kT", kb // 4), ("kpe2", kb // 4), "qn", "qr", "maskb", "ident"), writes=pregs(pb))

        if hd > 0:
            finalize_head(hd - 1)
        sc.add("dve", lambda h: h.memset(accD, 0.0), writes=("accD",))
        qk(0)
        qk(1)
        for kb in range(64):
            G = kb // 8
            pb = (kb % 3) * 2
            pt = pT[kb % 3]
            ptr = ("pT", kb % 3)
            if kb + 2 < 64:
                qk(kb + 2)
            sc.add("act", lambda h, G=G, pb=pb, pt=pt: h.activation(out=pt[:, G * 128:1024], in_=pair(pb)[:, G * 128:1024], func=AF.Exp),
                   reads=pregs(pb), writes=(ptr,))
            if G < 4:
                prng = [(0, G * 128, 512, 0), (1, 512, 1024, 512)]
            else:
                prng = [(1, G * 128, 1024, 512)]

            def pv(h, kb=kb, pt=pt, prng=prng):
                ins = None
                for half, c0, c1, off in prng:
                    ins = h.matmul(ps[:, 6 + half, c0 - off:c1 - off], vtok[:, kb, :], pt[:, c0:c1], start=(kb == 0),
                                   stop=(kb == (31 if half == 0 else 63)))
                return ins
            sc.add("pe", pv, reads=(ptr, ("vaug", kb // 4)), writes=pregs(6, 2))
            sc.add("dve", lambda h, G=G, pt=pt: h.tensor_tensor(out=accD[:, G * 128:1024], in0=accD[:, G * 128:1024], in1=pt[:, G * 128:1024], op=ALU.add),
                   reads=(ptr, "accD"), writes=("accD",))
    finalize_head(NHEAD - 1)
    dump(3, yb[:, 0, :], [("yb", 0)])
    dump(4, yb[:, 5, :], [("yb", 5)])

    sc.barrier()
    xf32 = v3(A1, 0, 16, 1024, F32)

    def fn_xf(h, sem):
        for q in range(4):
            h.dma_start(out=v2(A1, q * 8192, 8192, F32), in_=xoT_d[:, q * 4096:(q + 1) * 4096]).then_inc(sem, 16)
    sc.dma("sp", fn_xf, "xf32", 4, writes=("xf32",))
    for k in range(16):
        sc.add("dve", lambda h, k=k: h.scalar_tensor_tensor(out=hT[:, k, :], in0=xf32[:, k, :], scalar=small[:, SM_GPRE + k:SM_GPRE + k + 1],
                                                          in1=rstd_o[:], op0=ALU.mult, op1=ALU.mult),
               reads=("xf32", "rstd_o", "small"), writes=(("hT", k),))
    sc.barrier()
    zt = [v2(A5, 10240 + i * 1024, 1024) for i in range(2)]
    ws3 = WStream("w3", [v2(A5, i * PIECE, PIECE) for i in range(4)],
                  [PC["zb"] + c for c in range(16)] + [PC["u"] + c for c in range(16)] + [PC["za"] + c for c in range(16)]
                  + [PC["v"] + i for i in range(16)])
    pcnt = [0]

    def next_pair():
        b = (pcnt[0] % 4) * 2
        pcnt[0] += 1
        return b

    def proj_chunk(ws, act3, act_regs):
        w, wr = ws.next()
        w3 = w.rearrange("p (k c) -> p k c", k=16)
        b = next_pair()
        sc.add("pe", lambda h, b=b, w3=w3: mm32(b, w3, act3, h), reads=(wr,) + tuple(act_regs), writes=pregs(b))
        return b

    for c in range(16):
        b = proj_chunk(ws3, hT, HT)
        z = zt[c % 2]
        sc.add("act", lambda h, b=b, z=z: h.activation(out=z, in_=pair(b), func=AF.Silu), reads=pregs(b), writes=(("zt", c % 2),))
        sc.add("dve", lambda h, c=c, z=z: h.tensor_tensor(out=yb[:, c, :], in0=yb[:, c, :], in1=z, op=ALU.mult),
               reads=(("yb", c), ("zt", c % 2)), writes=(("yb", c),))
    ya = v3(A1, 0, 16, 1024)
    vg = v3(A1, 16384, 8, 2048)
    lnG = v2(A1, 32768, 4096, F32)
    lnB = v2(A1, 36864, 4096, F32)
    bsb = v3(SCR, 0, 16, 128, F32)
    wsTb = v3(SCR, 4096, 16, 128)
    vlnb = [v2(SCR, 6144, 2048), v2(A5, 8192, 2048)]
    t32 = v2(A5, 12288, 2048, F32)

    def ld_rows(dst, off, key):
        sc.dma("sp", lambda h, sem: h.dma_start(out=dst, in_=rows_d[0:1, off:off + 2048].to_broadcast([P, 2048])).then_inc(sem, 16),
               key, 1, writes=(key,))
    ld_rows(lnG, ROW_LNG, "lnG")
    ld_rows(lnB, ROW_LNB, "lnB")
    ld_rows(v2(SCR, 0, 4096, F32), ROW_BS, "bsb")
    sc.dma("pool", lambda h, sem: h.dma_start(out=v2(SCR, 4096, 2048), in_=wsT_d[:, :]).then_inc(sem, 16), "wsTb", 1, writes=("wsTb",))
    for g in range(16):
        sc.add("dve", lambda h, g=g: h.tensor_tensor(out=wsTb[:, g, :], in0=wsTb[:, g, :], in1=tri[:], op=ALU.mult),
               reads=("wsTb", "tri"), writes=("wsTb",))
    for c in range(16):
        b = proj_chunk(ws3, hT, HT)
        sc.add("act", lambda h, b=b, c=c: h.activation(out=ya[:, c, :], in_=pair(b), func=AF.Gelu_apprx_tanh), reads=pregs(b), writes=(("ya", c),))
    for c in range(16):
        b = proj_chunk(ws3, hT, HT)
        z = zt[c % 2]
        sc.add("act", lambda h, b=b, z=z: h.activation(out=z, in_=pair(b), func=AF.Silu), reads=pregs(b), writes=(("zt", c % 2),))
        sc.add("dve", lambda h, c=c, z=z: h.tensor_tensor(out=ya[:, c, :], in0=ya[:, c, :], in1=z, op=ALU.mult),
               reads=(("ya", c), ("zt", c % 2)), writes=(("ya", c),))
    s1 = stat[:, 0:32]
    s2 = stat[:, 32:64]
    mean = stat[:, 64:72]
    var = stat[:, 72:80]
    ex2 = stat[:, 80:88]
    sc.add("dve", lambda h: h.memset(stat[:, 0:88], 0.0), writes=("s1", "s2", "mean", "var", "ex2"))
    for grp in range(4):
        for kq in range(4):
            w, wr = ws3.next()
            w3 = w.rearrange("p (k c) -> p k c", k=4)

            def mm_vt(h, w3=w3, kq=kq):
                ins = None
                for j in range(8):
                    for kk in range(4):
                        ins = h.matmul(ps[:, j, :], hT[:, kq * 4 + kk, j * 128:(j + 1) * 128], w3[:, kk, :],
                                       start=(kq == 0 and kk == 0), stop=(kq == 3 and kk == 3))
                return ins
            sc.add("pe", mm_vt, reads=(wr,) + HT, writes=pregs(0, 8))
        for j in range(8):
            sc.add("act", lambda h, j=j, grp=grp: h.activation(out=vg[:, j, grp * 512:(grp + 1) * 512], in_=ps[:, j, :], func=AF.Gelu_apprx_tanh,
                                                             accum_out=s1[:, j * 4 + grp:j * 4 + grp + 1]),
                   reads=("ps%d" % j, "s1"), writes=(("vg", j, grp), "s1"))
            sc.add("act", lambda h, j=j, grp=grp: h.activation(out=vlnb[0][:, 0:512], in_=vg[:, j, grp * 512:(grp + 1) * 512], func=AF.Square,
                                                             accum_out=s2[:, j * 4 + grp:j * 4 + grp + 1]),
                   reads=(("vg", j, grp), "s2"), writes=(("vln", 0), "s2"))
    sc.add("dve", lambda h: h.reduce_sum(out=mean, in_=s1.rearrange("p (j g) -> p j g", g=4), axis=AX.X), reads=("s1",), writes=("mean",))
    sc.add("dve", lambda h: h.reduce_sum(out=ex2, in_=s2.rearrange("p (j g) -> p j g", g=4), axis=AX.X), reads=("s2",), writes=("ex2",))
    sc.add("dve", lambda h: h.tensor_scalar(out=mean, in0=mean, scalar1=1.0 / 2048, scalar2=None, op0=ALU.mult), reads=("mean",), writes=("mean",))
    sc.add("dve", lambda h: h.tensor_tensor(out=var, in0=mean, in1=mean, op=ALU.mult), reads=("mean",), writes=("var",))
    sc.add("dve", lambda h: h.scalar_tensor_tensor(out=var, in0=ex2, scalar=1.0 / 2048, in1=var, op0=ALU.mult, op1=ALU.subtract),
           reads=("ex2", "var"), writes=("var",))
    sc.add("act", lambda h: h.activation(out=var, in_=var, func=AF.Sqrt, bias=EPS, scale=1.0), reads=("var",), writes=("var",))
    sc.add("dve", lambda h: h.reciprocal(out=var, in_=var), reads=("var",), writes=("var",))
    VGJ = lambda j: tuple(("vg", j, g) for g in range(4))

    def do_ln(j):
        vl = vlnb[j % 2]
        vr = ("vln", j % 2)
        sc.add("dve", lambda h, j=j, vl=vl: h.scalar_tensor_tensor(out=vl, in0=vg[:, j, :], scalar=mean[:, j:j + 1], in1=lnG, op0=ALU.subtract, op1=ALU.mult),
               reads=VGJ(j) + ("mean", "lnG"), writes=(vr,))
        sc.add("dve", lambda h, j=j, vl=vl: h.scalar_tensor_tensor(out=vl, in0=vl, scalar=var[:, j:j + 1], in1=lnB, op0=ALU.mult, op1=ALU.add),
               reads=(vr, "var", "lnB"), writes=(vr,))

    do_ln(0)
    for j in range(8):
        if j + 1 < 8:
            do_ln(j + 1)
        vl = vlnb[j % 2]
        vr = ("vln", j % 2)
        for gq in range(4):
            def mm_sp(h, gq=gq, vl=vl):
                ins = None
                for gi in range(4):
                    g = gq * 4 + gi
                    ins = h.matmul(ps[:, gq, gi * 128:(gi + 1) * 128], vl[:, g * 128:(g + 1) * 128], wsTb[:, g, :], start=True, stop=True)
                return ins
            sc.add("pe", mm_sp, reads=(vr, "wsTb"), writes=pregs(gq, 1))
            t32v = t32[:, (gq % 2) * 512:(gq % 2) * 512 + 512].rearrange("p (a b) -> p a b", a=4)
            tr_ = ("t32", gq % 2)
            sc.add("dve", lambda h, gq=gq, t32v=t32v: h.tensor_tensor(out=t32v, in0=ps[:, gq, :].rearrange("p (a b) -> p a b", a=4),
                                                                    in1=bsb[:, gq * 4:(gq + 1) * 4, :], op=ALU.add),
                   reads=pregs(gq, 1) + ("bsb",), writes=(tr_,))
            sc.add("dve", lambda h, gq=gq, j=j, t32v=t32v: h.tensor_tensor(out=ya[:, gq * 4:(gq + 1) * 4, j * 128:(j + 1) * 128],
                                                                         in0=ya[:, gq * 4:(gq + 1) * 4, j * 128:(j + 1) * 128], in1=t32v, op=ALU.mult),
                   reads=(tr_,) + tuple(("ya", gq * 4 + i) for i in range(4)), writes=tuple(("ya", gq * 4 + i) for i in range(4)))
    dump(5, ya[:, 0, :], [("ya", 0)])

    sc.barrier()
    memn = v3(A1, 32768, 16, 256)
    kmT = v3(A1, 36864, 16, 256)
    ym = v3(A1, 16384, 16, 1024)
    vm = v3(SCR, 0, 2, 2048)
    msq = v3(SCR, 4096, 16, 256)
    qm = v3(SCR, 4096, 4, 1024)
    ztm = v2(A5, 8192, 1024)
    pm = v3(A5, 9216, 2, 1024)
    rs = v2(A5, 11264, 2048, F32)
    rstd_m = v2(A5, 13312, 512, F32)
    tmpm = v2(A5, 13824, 512, F32)
    MSC = float(512.0 ** -0.5)
    plist = [PC["memk"] + c for c in range(16)] + [PC["memv"] + i for i in range(16)]
    for hm in range(4):
        plist += [PC["qm"] + hm * 4 + dc for dc in range(4)]
        plist += [PC["zm"] + hm * 4 + dc for dc in range(4)]
    wsm = WStream("wm", [v2(A5, i * PIECE, PIECE) for i in range(4)], plist)
    sc.dma("pool", lambda h, sem: h.dma_start(out=v2(A1, 32768, 4096), in_=memT[:, :]).then_inc(sem, 16), "memn", 1, writes=("memn",))
    sc.add("dve", lambda h: h.tensor_tensor(out=msq, in0=memn, in1=memn, op=ALU.mult), reads=("memn",), writes=("msq",))

    def mm_ssm(h):
        ins = None
        for k in range(16):
            ins = h.matmul(ps[:, 0, 0:256], ones[:], msq[:, k, :], start=(k == 0), stop=(k == 15))
        return ins
    sc.add("pe", mm_ssm, reads=("msq", "ones"), writes=pregs(0, 1))
    rsqrt_bc(ps[:, 0, 0:256], rstd_m, float(D), pregs(0, 1), "rstd_m", tmpm, "tmpm")
    for k in range(16):
        sc.add("dve", lambda h, k=k: h.scalar_tensor_tensor(out=memn[:, k, :], in0=memn[:, k, :], scalar=small[:, SM_MEMG + k:SM_MEMG + k + 1],
                                                          in1=rstd_m, op0=ALU.mult, op1=ALU.mult),
               reads=("memn", "rstd_m", "small"), writes=("memn",))
    for c in range(16):
        w, wr = wsm.next()
        w3 = w.rearrange("p (k c) -> p k c", k=16)
        bank = 1 + c % 3

        def mm_km(h, w3=w3, bank=bank):
            ins = None
            for k in range(16):
                ins = h.matmul(ps[:, bank, 0:256], w3[:, k, :], memn[:, k, :], start=(k == 0), stop=(k == 15))
            return ins
        sc.add("pe", mm_km, reads=(wr, "memn"), writes=pregs(bank, 1))
        sc.add("act", lambda h, c=c, bank=bank: h.activation(out=kmT[:, c, :], in_=ps[:, bank, 0:256], func=AF.Copy), reads=pregs(bank, 1), writes=("kmT",))
    for grp in range(4):
        for kq in range(4):
            w, wr = wsm.next()
            w3 = w.rearrange("p (k c) -> p k c", k=4)

            def mm_vm(h, w3=w3, kq=kq):
                ins = None
                for mb in range(2):
                    for kk in range(4):
                        ins = h.matmul(ps[:, 4 + mb, :], memn[:, kq * 4 + kk, mb * 128:(mb + 1) * 128], w3[:, kk, :],
                                       start=(kq == 0 and kk == 0), stop=(kq == 3 and kk == 3))
                return ins
            sc.add("pe", mm_vm, reads=(wr, "memn"), writes=pregs(4, 2))
        for mb in range(2):
            sc.add("dve", lambda h, mb=mb, grp=grp: h.tensor_copy(out=vm[:, mb, grp * 512:(grp + 1) * 512], in_=ps[:, 4 + mb, :]),
                   reads=pregs(4 + mb, 1), writes=("vm",))
    sc.barrier()
    for hm in range(4):
        for dc in range(4):
            w, wr = wsm.next()
            w3 = w.rearrange("p (k c) -> p k c", k=16)
            b = (dc % 2) * 2
            sc.add("pe", lambda h, b=b, w3=w3: mm32(b, w3, hT, h), reads=(wr,) + HT, writes=pregs(b))
            sc.add("act", lambda h, b=b, dc=dc: h.activation(out=qm[:, dc, :], in_=pair(b), func=AF.Copy, scale=MSC), reads=pregs(b), writes=(("qm", dc),))
        QM = tuple(("qm", dc) for dc in range(4))
        for mb in range(2):
            def mm_sm(h, hm=hm, mb=mb):
                ins = None
                for half in range(2):
                    for dc in range(4):
                        ins = h.matmul(ps[:, 4 + 2 * mb + half, :], kmT[:, hm * 4 + dc, mb * 128:(mb + 1) * 128], qm[:, dc, half * 512:(half + 1) * 512],
                                       start=(dc == 0), stop=(dc == 3))
                return ins
            sc.add("pe", mm_sm, reads=QM + ("kmT",), writes=pregs(4 + 2 * mb))
            sc.add("act", lambda h, mb=mb: h.activation(out=pm[:, mb, :], in_=pair(4 + 2 * mb), func=AF.Exp), reads=pregs(4 + 2 * mb), writes=(("pm", mb),))
        PM = (("pm", 0), ("pm", 1))

        def mm_rs(h):
            ins = None
            for half in range(2):
                for mb in range(2):
                    ins = h.matmul(ps[:, half, :], ones[:], pm[:, mb, half * 512:(half + 1) * 512], start=(mb == 0), stop=(mb == 1))
            return ins
        sc.add("pe", mm_rs, reads=PM + ("ones",), writes=pregs(0))
        sc.add("dve", lambda h: h.reciprocal(out=rs, in_=pair(0)), reads=pregs(0), writes=("rs",))
        for dc in range(4):
            c = hm * 4 + dc
            w, wr = wsm.next()
            w3 = w.rearrange("p (k c) -> p k c", k=16)
            sc.add("pe", lambda h, w3=w3: mm32(2, w3, hT, h), reads=(wr,) + HT, writes=pregs(2))
            sc.add("act", lambda h: h.activation(out=ztm, in_=pair(2), func=AF.Silu), reads=pregs(2), writes=("ztm",))
            yb_ = 4 + 2 * (dc % 2)

            def mm_ym(h, c=c, yb_=yb_):
                ins = None
                for half in range(2):
                    for mb in range(2):
                        ins = h.matmul(ps[:, yb_ + half, :], vm[:, mb, c * 128:(c + 1) * 128], pm[:, mb, half * 512:(half + 1) * 512],
                                       start=(mb == 0), stop=(mb == 1))
                return ins
            sc.add("pe", mm_ym, reads=PM + ("vm",), writes=pregs(yb_))
            sc.add("dve", lambda h, c=c, yb_=yb_: h.tensor_tensor(out=ym[:, c, :], in0=pair(yb_), in1=rs, op=ALU.mult),
                   reads=pregs(yb_) + ("rs",), writes=(("ym", c),))
            sc.add("dve", lambda h, c=c: h.tensor_tensor(out=ym[:, c, :], in0=ym[:, c, :], in1=ztm, op=ALU.mult),
                   reads=(("ym", c), "ztm"), writes=(("ym", c),))
    dump(6, ym[:, 0, :], [("ym", 0)])

    sc.barrier()

    def mg(c):
        return v2(A1, 32768 + c * 1024, 1024) if c < 8 else v2(SCR, (c - 8) * 1024, 1024)
    gt = v2(A5, 8192, 2048, F32)
    acc = v2(A5, 10240, 2048, F32)
    tmp4 = v2(A5, 12288, 2048, F32)
    plist = []
    for c in range(16):
        for n in range(3):
            plist += [PC["gate"] + n * 16 + c, PC["branch"] + n * 16 + c]
    ws4 = WStream("w4", [v2(A5, i * PIECE, PIECE) for i in range(4)], plist)
    ysrc = [(ya, "ya"), (yb, "yb"), (ym, "ym")]
    for c in range(16):
        for n in range(3):
            b = proj_chunk(ws4, hT, HT)
            sc.add("act", lambda h, b=b, n=n, c=c: h.activation(out=gt, in_=pair(b), func=AF.Sigmoid,
                                                              bias=small[:, SM_BGATE + n * 16 + c:SM_BGATE + n * 16 + c + 1], scale=1.0),
                   reads=pregs(b) + ("small",), writes=("gt",))
            ysb, yname = ysrc[n]
            b2 = proj_chunk(ws4, ysb, tuple((yname, k) for k in range(16)))
            if n == 0:
                sc.add("dve", lambda h, b2=b2: h.tensor_tensor(out=acc, in0=pair(b2), in1=gt, op=ALU.mult), reads=pregs(b2) + ("gt",), writes=("acc",))
            else:
                sc.add("dve", lambda h, b2=b2: h.tensor_tensor(out=tmp4, in0=pair(b2), in1=gt, op=ALU.mult), reads=pregs(b2) + ("gt",), writes=("tmp4",))
                if n == 1:
                    sc.add("dve", lambda h: h.tensor_tensor(out=acc, in0=acc, in1=tmp4, op=ALU.add), reads=("acc", "tmp4"), writes=("acc",))
                else:
                    sc.add("dve", lambda h, c=c: h.tensor_tensor(out=mg(c), in0=acc, in1=tmp4, op=ALU.add), reads=("acc", "tmp4"), writes=(("mg", c),))
    dump(7, mg(0), [("mg", 0)])

    sc.barrier()
    gpb = v2(A2, 0, 4096, F32)
    junk = v2(A2, 4096, 1024, F32)
    xob = [v2(A3, i * 4096, 4096, F32) for i in range(2)]
    yo = [v2(A3, 8192 + i * 4096, 4096, F32) for i in range(2)]
    sc.dma("sp", lambda h, sem: h.dma_start(out=gpb, in_=rows_d[0:1, ROW_GPOST:ROW_GPOST + 2048].to_broadcast([P, 2048])).then_inc(sem, 16),
           "gpb", 1, writes=("gpb",))
    for i in range(16):
        load_piece(PC["out"] + i, v2(A1, i * PIECE, PIECE), ("wo", i), "wo%d" % i)

    def wo(grp, kq):
        return v3(A1, (grp * 4 + kq) * PIECE, 4, 512)
    MG = tuple(("mg", c) for c in range(16))
    for j in range(8):
        pj = j % 2
        sc.dma("sp", lambda h, sem, j=j, pj=pj: h.dma_start(out=xob[pj], in_=xo_d[j * 128:(j + 1) * 128, :]).then_inc(sem, 16),
               "xob%d" % pj, 1, writes=(("xob", pj),))
        sc.add("dve", lambda h, pj=pj: h.memset(stat[:, pj * 4:pj * 4 + 4], 0.0), writes=tuple(("ss5", pj * 4 + g) for g in range(4)))
        for grp in range(4):
            bank = pj * 4 + grp

            def mm_o(h, j=j, grp=grp, bank=bank):
                ins = None
                for k in range(16):
                    ins = h.matmul(ps[:, bank, :], mg(k)[:, j * 128:(j + 1) * 128], wo(grp, k // 4)[:, k % 4, :], start=(k == 0), stop=(k == 15))
                return ins
            sc.add("pe", mm_o, reads=MG + tuple(("wo", grp * 4 + q) for q in range(4)), writes=pregs(bank, 1))
            sc.add("act", lambda h, bank=bank: h.activation(out=junk, in_=ps[:, bank, :], func=AF.Square, accum_out=stat[:, bank:bank + 1]),
                   reads=pregs(bank, 1) + (("ss5", bank),), writes=("junk", ("ss5", bank)))
        c8, c10, c12 = 8 + pj, 10 + pj, 12 + pj
        sc.add("dve", lambda h, pj=pj, c8=c8: h.reduce_sum(out=stat[:, c8:c8 + 1], in_=stat[:, pj * 4:pj * 4 + 4], axis=AX.X),
               reads=tuple(("ss5", pj * 4 + g) for g in range(4)), writes=(("st5a", pj),))
        sc.add("act", lambda h, c8=c8, c10=c10: h.activation(out=stat[:, c10:c10 + 1], in_=stat[:, c8:c8 + 1], func=AF.Sqrt, bias=EPS, scale=1.0 / D),
               reads=(("st5a", pj),), writes=(("st5b", pj),))
        sc.add("dve", lambda h, c10=c10, c12=c12: h.reciprocal(out=stat[:, c12:c12 + 1], in_=stat[:, c10:c10 + 1]), reads=(("st5b", pj),), writes=(("st5c", pj),))
        for grp in range(4):
            bank = pj * 4 + grp
            sc.add("dve", lambda h, grp=grp, bank=bank, pj=pj, c12=c12: h.scalar_tensor_tensor(
                out=yo[pj][:, grp * 512:(grp + 1) * 512], in0=ps[:, bank, :], scalar=stat[:, c12:c12 + 1], in1=gpb[:, grp * 512:(grp + 1) * 512],
                op0=ALU.mult, op1=ALU.mult), reads=pregs(bank, 1) + (("st5c", pj), "gpb"), writes=(("yo", pj),))
        sc.add("dve", lambda h, pj=pj: h.tensor_tensor(out=yo[pj], in0=yo[pj], in1=xob[pj], op=ALU.add), reads=(("yo", pj), ("xob", pj)), writes=(("yo", pj),))
        sc.dma("sp", lambda h, sem, j=j, pj=pj: h.dma_start(out=y_d[j * 128:(j + 1) * 128, :], in_=yo[pj]).then_inc(sem, 16),
               "yout%d" % pj, 1, reads=(("yo", pj),))

    sc.finalize()
    build.last_sc = sc
    sem_names = ["eng:" + e for e in Sched.ENGS] + ["dma:" + k for k in sc.dma_keys]
    sems = {}
    for i, s in enumerate(sem_names):
        sems[s] = es.enter_context(nc.semaphore("s%d" % i))
    final = [("dma:yout0", 64), ("dma:yout1", 64)]
    for seg in dbg_segs:
        final.append(("dma:dbg%d" % seg, 16))
    with nc.Block() as block:
        sc.emit(nc, block, sems, final)
    es.close()
    return nc


def kernel(**inputs):
    per_core = prep_inputs(inputs)
    nc = build()
    in_maps = [d for _, d in per_core]
    res = run_bass_kernel_spmd(nc, in_maps, core_ids=list(range(NCORE)))
    out = np.zeros((1, S, D), np.float32)
    for (idx, _), r in zip(per_core, res.results):
        out[0, idx] = r["y"]
    return out
```
